# Optimizing a Trainium2 kernel written in Bass

```python
import jax, jax.numpy as jnp
from jax import lax
import numpy as np

D_MODEL = 1024
BATCH = 8
SEQ = 4096
DEPTH = 2

N_MEM = 256
GLA_HEADS = 6
GLA_DK = 64
GLA_DV = 128
GLA_K = GLA_HEADS * GLA_DK
GLA_V = GLA_HEADS * GLA_DV
GLA_GATE_RANK = 16
GLA_TAU = 16.0
GLA_CHUNK = 64
SB_HEADS = 12
SB_DIM = 64
SB_W = SB_HEADS * SB_DIM
SB_BLOCK = 128
MEM_HEADS = 4
MEM_DIM = 64
MEM_W = MEM_HEADS * MEM_DIM
PEER_HEADS = 8
PEER_KEYS = 128
PEER_EXPERTS = PEER_KEYS * PEER_KEYS
PEER_TOPK = 16
PEER_QHALF = 128
PEER_TOKEN_BLOCK = 128
N_A_LAYERS = DEPTH // 2
N_B_LAYERS = DEPTH - N_A_LAYERS
DEEPNORM_ALPHA = (2.0 * DEPTH) ** 0.25
DEEPNORM_BETA = (8.0 * DEPTH) ** -0.25
EPS = 1e-5
A_IN_SPLITS = [GLA_K, 2 * GLA_K, 2 * GLA_K + GLA_V, 2 * GLA_K + 2 * GLA_V, 2 * GLA_K + 2 * GLA_V + GLA_GATE_RANK]
A_IN_W = 2 * GLA_K + 2 * GLA_V + GLA_GATE_RANK + MEM_W
B_IN_W = SB_W + MEM_W

kernel_name = "yoco_gla_stickbreaking_peer_trunk"


def layer_norm(x, g, b):
    xf = x.astype(jnp.float32)
    mu = jnp.mean(xf, axis=-1, keepdims=True)
    var = jnp.mean(jnp.square(xf - mu), axis=-1, keepdims=True)
    return ((xf - mu) * lax.rsqrt(var + EPS) * g + b).astype(x.dtype)


def to_heads(t, n_heads, d):
    B, T, _ = t.shape
    return t.reshape(B, T, n_heads, d).transpose(0, 2, 1, 3)


def from_heads(t):
    B, H, T, d = t.shape
    return t.transpose(0, 2, 1, 3).reshape(B, T, H * d)


def memory_attention(qm, mem, w_mem_kv):
    B, T, _ = qm.shape
    kv = mem @ w_mem_kv
    km, vm = jnp.split(kv, [MEM_W], axis=-1)
    q = qm.reshape(B, T, MEM_HEADS, MEM_DIM)
    k = km.reshape(B, -1, MEM_HEADS, MEM_DIM)
    v = vm.reshape(B, -1, MEM_HEADS, MEM_DIM)
    s = jnp.einsum('bthd,bmhd->bhtm', q, k).astype(jnp.float32) * (MEM_DIM ** -0.5)
    p = jax.nn.softmax(s, axis=-1).astype(v.dtype)
    return jnp.einsum('bhtm,bmhd->bthd', p, v).reshape(B, T, MEM_W)


def gla_chunked(q, k, v, log_g):
    B, H, T, dk = q.shape
    dv = v.shape[-1]
    n = T // GLA_CHUNK
    c = lambda t: t.astype(jnp.float32).reshape(B, H, n, GLA_CHUNK, t.shape[-1])
    q, k, v, log_g = c(q), c(k), c(v), c(log_g)
    b = jnp.cumsum(log_g, axis=-2)
    b_last = b[..., -1, :]
    q_dec = q * jnp.exp(b)
    k_inv = k * jnp.exp(-b)
    causal = jnp.tril(jnp.ones((GLA_CHUNK, GLA_CHUNK), dtype=bool))
    attn = jnp.where(causal, jnp.einsum('bhncd,bhnsd->bhncs', q_dec, k_inv), 0.0)
    o_intra = jnp.einsum('bhncs,bhnsv->bhncv', attn, v)
    k_to_end = k * jnp.exp(b_last[..., None, :] - b)
    du = jnp.einsum('bhncd,bhncv->bhndv', k_to_end, v)

    def step(S, inp):
        decay, inc = inp
        return decay[..., None] * S + inc, S

    S0 = jnp.zeros((B, H, dk, dv), jnp.float32)
    _, S_start = lax.scan(step, S0, (jnp.moveaxis(jnp.exp(b_last), 2, 0), jnp.moveaxis(du, 2, 0)))
    S_start = jnp.moveaxis(S_start, 0, 2)
    o_inter = jnp.einsum('bhncd,bhndv->bhncv', q_dec, S_start)
    return (o_intra + o_inter).reshape(B, H, T, dv)


def gla_layer_mixer(x, mem, w_in, w_gate2, b_gate, head_g, w_mem_kv, w_out):
    proj = x @ w_in
    q, k, v, r, glr, qm = jnp.split(proj, A_IN_SPLITS, axis=-1)
    log_g = jax.nn.log_sigmoid((glr @ w_gate2 + b_gate).astype(jnp.float32)) / GLA_TAU
    o = gla_chunked(to_heads(q, GLA_HEADS, GLA_DK) * (GLA_DK ** -0.5),
                    to_heads(k, GLA_HEADS, GLA_DK),
                    to_heads(v, GLA_HEADS, GLA_DV),
                    to_heads(log_g, GLA_HEADS, GLA_DK))
    o = o * lax.rsqrt(jnp.mean(jnp.square(o), axis=-1, keepdims=True) + EPS) * head_g[None, :, None, :]
    o = from_heads(o).astype(x.dtype) * jax.nn.silu(r)
    m = memory_attention(qm, mem, w_mem_kv)
    return jnp.concatenate([o, m], axis=-1) @ w_out


def stick_breaking(q, k, v):
    B, H, T, d = q.shape
    scale = d ** -0.5
    outs = []
    for i in range(T // SB_BLOCK):
        t0, t1 = i * SB_BLOCK, (i + 1) * SB_BLOCK
        z = jnp.einsum('bhtd,bhsd->bhts', q[:, :, t0:t1], k[:, :, :t1]).astype(jnp.float32) * scale
        t_pos = t0 + jnp.arange(SB_BLOCK)[:, None]
        s_pos = jnp.arange(t1)[None, :]
        mask = s_pos < t_pos
        log_not = jnp.where(mask, jax.nn.log_sigmoid(-z), 0.0)
        log_A = jax.nn.log_sigmoid(z) + lax.cumsum(log_not, axis=3, reverse=True) - log_not
        A = jnp.where(mask, jnp.exp(log_A), 0.0).astype(v.dtype)
        outs.append(jnp.einsum('bhts,bhsd->bhtd', A, v[:, :, :t1]))
    return jnp.concatenate(outs, axis=2)


def sb_layer_mixer(x, mem, k_sb, v_sb, w_in, w_mem_kv, w_out):
    q, qm = jnp.split(x @ w_in, [SB_W], axis=-1)
    o = from_heads(stick_breaking(to_heads(q, SB_HEADS, SB_DIM), k_sb, v_sb))
    m = memory_attention(qm, mem, w_mem_kv)
    return jnp.concatenate([o, m], axis=-1) @ w_out


def peer(x, w_q, subkeys, u_tab, v_tab):
    B, T, D = x.shape
    N = B * T
    xf = x.reshape(N, D)
    q = (xf @ w_q).reshape(N, PEER_HEADS, 2, PEER_QHALF)
    s = jnp.einsum('nhpd,hpkd->nhpk', q, subkeys).astype(jnp.float32)
    top_s, top_i = lax.top_k(s, PEER_TOPK)
    cand_s = (top_s[:, :, 0, :, None] + top_s[:, :, 1, None, :]).reshape(N, PEER_HEADS, PEER_TOPK * PEER_TOPK)
    cand_i = (top_i[:, :, 0, :, None] * PEER_KEYS + top_i[:, :, 1, None, :]).reshape(N, PEER_HEADS, PEER_TOPK * PEER_TOPK)
    best_s, best_pos = lax.top_k(cand_s, PEER_TOPK)
    idx = jnp.take_along_axis(cand_i, best_pos, axis=-1)
    g = jax.nn.softmax(best_s, axis=-1).astype(x.dtype)
    nb = N // PEER_TOKEN_BLOCK
    xb = xf.reshape(nb, PEER_TOKEN_BLOCK, D)
    ib = idx.reshape(nb, PEER_TOKEN_BLOCK, PEER_HEADS, PEER_TOPK)
    gb = g.reshape(nb, PEER_TOKEN_BLOCK, PEER_HEADS, PEER_TOPK)

    def apply_experts(args):
        xt, it, gt = args
        act = jax.nn.gelu(jnp.einsum('nd,nhkd->nhk', xt, u_tab[it]), approximate=False)
        return jnp.einsum('nhk,nhkd->nd', gt * act, v_tab[it])

    return lax.map(apply_experts, (xb, ib, gb)).reshape(B, T, D)


def setup_inputs(seed: int = 0) -> dict:
    key = jax.random.key(seed)
    ks = jax.random.split(key, 20)
    nrm = lambda k, shape, scale: jax.random.normal(k, shape, jnp.float32) * scale
    ds = D_MODEL ** -0.5
    beta = DEEPNORM_BETA
    a_col_scale = jnp.concatenate([
        jnp.ones((2 * GLA_K,), jnp.float32),
        jnp.full((GLA_V,), beta, jnp.float32),
        jnp.ones((GLA_V + GLA_GATE_RANK + MEM_W,), jnp.float32)])
    mem_col_scale = jnp.concatenate([jnp.ones((MEM_W,), jnp.float32), jnp.full((MEM_W,), beta, jnp.float32)])
    sb_col_scale = jnp.concatenate([jnp.ones((SB_W,), jnp.float32), jnp.full((SB_W,), beta, jnp.float32)])
    return {
        "x": nrm(ks[0], (BATCH, SEQ, D_MODEL), 1.0),
        "mem": nrm(ks[1], (BATCH, N_MEM, D_MODEL), 1.0),
        "a_w_in": nrm(ks[2], (N_A_LAYERS, D_MODEL, A_IN_W), ds) * a_col_scale,
        "a_w_gate2": nrm(ks[3], (N_A_LAYERS, GLA_GATE_RANK, GLA_K), GLA_GATE_RANK ** -0.5),
        "a_b_gate": nrm(ks[4], (N_A_LAYERS, GLA_K), 0.1),
        "a_head_g": 1.0 + nrm(ks[5], (N_A_LAYERS, GLA_HEADS, GLA_DV), 0.05),
        "a_w_mem_kv": nrm(ks[6], (N_A_LAYERS, D_MODEL, 2 * MEM_W), ds) * mem_col_scale,
        "a_w_out": nrm(ks[7], (N_A_LAYERS, GLA_V + MEM_W, D_MODEL), (GLA_V + MEM_W) ** -0.5 * beta),
        "b_w_in": nrm(ks[8], (N_B_LAYERS, D_MODEL, B_IN_W), ds),
        "b_w_mem_kv": nrm(ks[9], (N_B_LAYERS, D_MODEL, 2 * MEM_W), ds) * mem_col_scale,
        "b_w_out": nrm(ks[10], (N_B_LAYERS, SB_W + MEM_W, D_MODEL), (SB_W + MEM_W) ** -0.5 * beta),
        "sb_w_kv": nrm(ks[11], (D_MODEL, 2 * SB_W), ds) * sb_col_scale,
        "peer_w_q": nrm(ks[12], (DEPTH, D_MODEL, PEER_HEADS * 2 * PEER_QHALF), ds),
        "peer_subkeys": nrm(ks[13], (DEPTH, PEER_HEADS, 2, PEER_KEYS, PEER_QHALF), PEER_QHALF ** -0.5),
        "peer_u": nrm(ks[14], (DEPTH, PEER_EXPERTS, D_MODEL), ds),
        "peer_v": nrm(ks[15], (DEPTH, PEER_EXPERTS, D_MODEL), beta * PEER_HEADS ** -0.5),
        "ln_g": 1.0 + nrm(ks[16], (DEPTH, 2, D_MODEL), 0.05),
        "ln_b": nrm(ks[17], (DEPTH, 2, D_MODEL), 0.02),
    }


def reference(x, mem, a_w_in, a_w_gate2, a_b_gate, a_head_g, a_w_mem_kv, a_w_out,
              b_w_in, b_w_mem_kv, b_w_out, sb_w_kv, peer_w_q, peer_subkeys, peer_u, peer_v,
              ln_g, ln_b):
    h = x
    k_sb = None
    v_sb = None
    for layer in range(DEPTH):
        if layer < N_A_LAYERS:
            i = layer
            mix = gla_layer_mixer(h, mem, a_w_in[i], a_w_gate2[i], a_b_gate[i], a_head_g[i],
                                  a_w_mem_kv[i], a_w_out[i])
        else:
            if layer == N_A_LAYERS:
                ksb, vsb = jnp.split(h @ sb_w_kv, [SB_W], axis=-1)
                k_sb = to_heads(ksb, SB_HEADS, SB_DIM)
                v_sb = to_heads(vsb, SB_HEADS, SB_DIM)
            j = layer - N_A_LAYERS
            mix = sb_layer_mixer(h, mem, k_sb, v_sb, b_w_in[j], b_w_mem_kv[j], b_w_out[j])
        h = layer_norm(DEEPNORM_ALPHA * h + mix, ln_g[layer, 0], ln_b[layer, 0])
        ffn = peer(h, peer_w_q[layer], peer_subkeys[layer], peer_u[layer], peer_v[layer])
        h = layer_norm(DEEPNORM_ALPHA * h + ffn, ln_g[layer, 1], ln_b[layer, 1])
    return h
```

```python
import numpy as np
from contextlib import ExitStack
import concourse.bass as bass
import concourse.mybir as mybir
from concourse.bass_utils import run_bass_kernel_spmd

F32 = mybir.dt.float32
BF16 = mybir.dt.bfloat16
AF = mybir.ActivationFunctionType
ALU = mybir.AluOpType
AX = mybir.AxisListType

T = 4096
D = 1024
NT = T // 128
ALPHA = (2.0 * 2) ** 0.25
EPS = 1e-5


class Buf:
    __slots__ = ("t", "name", "excl", "lw", "rd")

    def __init__(self, t, name, excl=False):
        self.t = t
        self.name = name
        self.excl = excl
        self.lw = None
        self.rd = []

    def __getitem__(self, k):
        return self.t[k]


class Eng:
    def __init__(self, name, h):
        self.name = name
        self.h = h
        self.sem = None
        self.cnt = 0
        self.waited = {}


class FW:
    def __init__(self, nc, stack, n_dma_sems=32):
        self.nc = nc
        self.stack = stack
        self.E = {}
        for name, h in (("pe", nc.tensor), ("act", nc.scalar), ("dve", nc.vector),
                        ("pool", nc.gpsimd), ("sp", nc.sync)):
            e = Eng(name, h)
            e.sem = stack.enter_context(nc.semaphore("s_" + name))
            self.E[name] = e
        self.dma_sems = [stack.enter_context(nc.semaphore("d%d" % i)) for i in range(n_dma_sems)]
        self.dma_tot = [0] * n_dma_sems
        self.dma_ring = {"sp": list(range(0, n_dma_sems - 8)), "pool": list(range(n_dma_sems - 8, n_dma_sems))}
        self.dma_pos = {"sp": 0, "pool": 0}
        self.n_ins = 0
        self.ps_i = 0
        self.PS = []

    def sbuf(self, name, shape, dt):
        self.n_alloc = getattr(self, "n_alloc", 0) + 1
        t = self.stack.enter_context(self.nc.sbuf_tensor("sb%d_%s" % (self.n_alloc, name), list(shape), dt))
        return Buf(t, name)

    def psum(self, name, shape, dt):
        t = self.stack.enter_context(self.nc.psum_tensor(name, list(shape), dt))
        return Buf(t, name, excl=True)

    def next_ps(self):
        b = self.PS[self.ps_i]
        self.ps_i = (self.ps_i + 1) % len(self.PS)
        return b

    def _need(self, eng, deps, skip_same, out):
        best = {}
        for (k, v) in deps:
            if skip_same and k is eng.sem:
                continue
            if v > best.get(k, 0):
                best[k] = v
        for k, v in best.items():
            if eng.waited.get(k, 0) < v and out.get(k, 0) < v:
                out[k] = v

    def _wait(self, eng, deps, skip_same):
        need = {}
        self._need(eng, deps, skip_same, need)
        for k, v in need.items():
            eng.h.wait_ge(k, v)
            eng.waited[k] = v

    def _sync(self, e, reads, writes, is_pe):
        raw = []
        other = []
        for b in reads:
            if b.lw is not None:
                raw.append(b.lw)
            if b.excl:
                other.extend(b.rd)
        for b in writes:
            if b.lw is not None:
                other.append(b.lw)
            other.extend(b.rd)
        need = {}
        self._need(e, raw, is_pe, need)
        self._need(e, other, is_pe, need)
        items = list(need.items())
        for k, v in items[:-1]:
            e.h.wait_ge(k, v)
            e.waited[k] = v
        if items:
            k, v = items[-1]
            e.waited[k] = v
            return (k, v)
        return None

    def _mark(self, ev, reads, writes):
        for b in reads:
            if b.excl:
                b.lw = ev
                b.rd = []
            else:
                b.rd.append(ev)
                if len(b.rd) > 48:
                    b.rd = b.rd[-48:]
        for b in writes:
            b.lw = ev
            b.rd = []

    def op(self, eng, fn, reads=(), writes=()):
        e = self.E[eng]
        w = self._sync(e, reads, writes, eng == "pe")
        ins = fn(e.h)
        if w is not None:
            ins._wait_ge(w[0], w[1])
        e.cnt += 1
        ins.then_inc(e.sem, 1)
        self._mark((e.sem, e.cnt), reads, writes)
        self.n_ins += 1
        return ins

    def dma(self, out_ap, in_ap, reads=(), writes=(), q="sp", **kw):
        e = self.E[q]
        ring = self.dma_ring[q]
        i = ring[self.dma_pos[q]]
        self.dma_pos[q] = (self.dma_pos[q] + 1) % len(ring)
        sem = self.dma_sems[i]
        w = self._sync(e, reads, writes, False)
        if w is not None:
            e.h.wait_ge(w[0], w[1])
        if self.dma_tot[i] > 0:
            self._wait(e, [(sem, self.dma_tot[i])], False)
        ins = e.h.dma_start(out=out_ap, in_=in_ap, **kw)
        self.dma_tot[i] += 16
        ins.then_inc(sem, 16)
        self._mark((sem, self.dma_tot[i]), reads, writes)
        self.n_ins += 1
        return ins

    def barrier(self):
        evs = [(e.sem, e.cnt) for e in self.E.values() if e.cnt > 0]
        evs += [(s, t) for s, t in zip(self.dma_sems, self.dma_tot) if t > 0]
        for e in self.E.values():
            self._wait(e, evs, skip_same=True)


def make_consts():
    c = np.zeros((128, 768), np.float32)
    i = np.arange(128)
    c[:, 0:128] = np.eye(128, dtype=np.float32)
    c[:, 128:256] = (i[:, None] <= i[None, :]).astype(np.float32)
    c[:, 256:384] = (i[:, None] < i[None, :]).astype(np.float32)
    c[:, 384:512] = 1.0
    c[:, 512:640] = (i[:, None] >= i[None, :]).astype(np.float32)
    c[:, 640:768] = i[None, :].astype(np.float32)
    return c


def build_program(stop_after=4, ntiles=NT, start_at=1, ntb=1):
    nc = bass.Bass("TRN2", target_bir_lowering=False)
    dt_in = lambda name, shape: nc.dram_tensor(name, list(shape), F32, kind="ExternalInput").ap()
    x_d = dt_in("x", [T, D])
    mem_d = dt_in("mem", [256, D])
    a_w_in_d = dt_in("a_w_in", [D, 2576])
    a_wg_d = dt_in("a_wg", [32, 384])
    a_hg_d = dt_in("a_hg", [1, 768])
    a_wkv_d = dt_in("a_wkv", [D, 512])
    a_wo_d = dt_in("a_wo", [D, D])
    ln_g_d = dt_in("ln_g", [4, D])
    ln_b_d = dt_in("ln_b", [4, D])
    consts_d = dt_in("consts", [128, 768])
    sbkv_d = dt_in("sbkv", [D, 1536])
    b_w_in_d = dt_in("b_w_in", [D, D])
    b_wkv_d = dt_in("b_wkv", [D, 512])
    b_wo_d = dt_in("b_wo", [D, D])
    wq_d = [dt_in("wq%d" % l, [D, 2048]) for l in range(2)]
    skT_d = [dt_in("skT%d" % l, [16, 128, 128]) for l in range(2)]
    uh_d = [dt_in("uh%d" % l, [128, 128, D]) for l in range(2)]
    vh_d = [dt_in("vh%d" % l, [128, 128, D]) for l in range(2)]
    out_d = nc.dram_tensor("out", [T, D], F32, kind="ExternalOutput").ap()
    scr = lambda name, shape, dt: nc.dram_tensor(name, list(shape), dt, kind="Internal").ap()
    H = [None, scr("H1", [T, D], F32), scr("H2", [T, D], F32), scr("H3", [T, D], F32), None]
    rt_d = scr("RT", [T, RW], F32)
    ub_d = [scr("ub%d" % l, [128, 128, D], BF16) for l in range(2)]
    vb_d = [scr("vb%d" % l, [128, 128, D], BF16) for l in range(2)]
    H[start_at - 1] = x_d
    H[stop_after] = out_d

    with ExitStack() as st:
        fw = FW(nc, st)
        for i in range(8):
            fw.PS.append(fw.psum("ps%d" % i, [128, 512], F32))
        cst = fw.sbuf("cst", [128, 768], F32)
        fw.dma(cst[:], consts_d, writes=[cst])
        fw.barrier()
        if start_at <= 2 <= stop_after:
            precast_tables(fw, uh_d[0], vh_d[0], ub_d[0], vb_d[0])
        if start_at == 4:
            precast_tables(fw, uh_d[1], vh_d[1], ub_d[1], vb_d[1])
        for ph in range(start_at, stop_after + 1):
            if ph == 3 and stop_after >= 4:
                precast_tables(fw, uh_d[1], vh_d[1], ub_d[1], vb_d[1])
            if ph == 1:
                with ExitStack() as st1:
                    fw.stack = st1
                    phase_mixA(fw, nc, st1, cst, H[0], mem_d, a_w_in_d, a_wg_d, a_hg_d, a_wkv_d, a_wo_d,
                               ln_g_d[0:1, :], ln_b_d[0:1, :], H[1], None, ntiles)
                    fw.barrier()
                fw.stack = st
            elif ph in (2, 4):
                l = (ph - 2) // 2
                peer_route(fw, nc, cst, H[ph - 1], wq_d[l], skT_d[l], rt_d, ntiles)
                if not _DBG.get("route_only"):
                    peer_dense(fw, nc, cst, H[ph - 1], rt_d, ub_d[l], vb_d[l], ln_g_d[2 * l + 1:2 * l + 2, :],
                               ln_b_d[2 * l + 1:2 * l + 2, :], H[ph], ntiles, NTB=ntb)
            elif ph == 3:
                phase_mixB(fw, nc, cst, H[2], mem_d, sbkv_d, b_w_in_d, b_wkv_d, b_wo_d, ln_g_d[2:3, :], ln_b_d[2:3, :], H[3], ntiles)
        fw.barrier()
        print("n_ins", fw.n_ins)
    return nc


def load_bcast(fw, name, src_row, n):
    t = fw.sbuf(name, [128, n], F32)
    fw.dma(t[:], src_row.partition_broadcast(128), writes=[t])
    return t


def transpose_to(fw, dst_buf, dst_ap_fn, src_buf, src_ap_fn, n, ident, rows=128, cols=128,
                 evac=("act", "dve")):
    j = 0
    gi = 0
    while j < n:
        g = min(4, n - j)
        ps = fw.next_ps()
        for a in range(g):
            fw.op("pe", lambda e: e.transpose(ps[0:cols, a * 128:a * 128 + rows], src_ap_fn(j + a),
                                              ident[0:rows, 0:rows]),
                  reads=[src_buf], writes=[ps])
        eng = evac[gi % len(evac)]
        src = ps[0:cols, 0:g * 128].rearrange("p (a r) -> p a r", a=g)[:, :, 0:rows]
        if eng == "act":
            fw.op("act", lambda e: e.copy(out=dst_ap_fn(j, g), in_=src), reads=[ps], writes=[dst_buf])
        else:
            fw.op(eng, lambda e: e.tensor_copy(out=dst_ap_fn(j, g), in_=src), reads=[ps], writes=[dst_buf])
        j += g
        gi += 1


def mem_kv(fw, cst, mem_d, wkv_d, pfx):
    ident = cst[:, 0:128]
    kmT4 = fw.sbuf(pfx + "kmT4", [64, 4, 256], F32)
    vm = fw.sbuf(pfx + "vm", [128, 2, 256], F32)
    outer = fw.stack
    with ExitStack() as tmp:
        fw.stack = tmp
        memt = fw.sbuf(pfx + "memt", [128, 2, D], F32)
        fw.dma(memt[:], mem_d.rearrange("(c p) d -> p c d", p=128), writes=[memt])
        wkv = fw.sbuf(pfx + "wkv", [128, 8, 512], F32)
        fw.dma(wkv[:], wkv_d.rearrange("(k p) n -> p k n", p=128), writes=[wkv])
        memT = fw.sbuf(pfx + "memT", [128, 8, 256], F32)
        for c in range(2):
            transpose_to(fw, memT, lambda j, g: memT[:, j:j + g, c * 128:(c + 1) * 128],
                         memt, lambda j: memt[:, c, j * 128:(j + 1) * 128], 8, ident)
        for h in range(4):
            ps = fw.next_ps()
            for k in range(8):
                fw.op("pe", lambda e: e.matmul(ps[0:64, 0:256], lhsT=wkv[:, k, h * 64:(h + 1) * 64],
                                               rhs=memT[:, k, :], start=(k == 0), stop=(k == 7)),
                      reads=[wkv, memT], writes=[ps])
            fw.op("dve", lambda e: e.tensor_copy(out=kmT4[:, h, :], in_=ps[0:64, 0:256]), reads=[ps], writes=[kmT4])
        for c in range(2):
            ps = fw.next_ps()
            for k in range(8):
                fw.op("pe", lambda e: e.matmul(ps[:, 0:256], lhsT=memT[:, k, c * 128:(c + 1) * 128],
                                               rhs=wkv[:, k, 256:512], start=(k == 0), stop=(k == 7)),
                      reads=[wkv, memT], writes=[ps])
            fw.op("dve", lambda e: e.tensor_copy(out=vm[:, c, :], in_=ps[:, 0:256]), reads=[ps], writes=[vm])
        fw.barrier()
    fw.stack = outer
    return kmT4, vm


def mem_attention(fw, cst, qm_buf, qm_ap, kmT4, vm, om, om_off, W):
    ident = cst[:, 0:128]
    qmT4, p_sb, pT, st = W["qmT4"], W["p_sb"], W["pT"], W["mst"]
    transpose_to(fw, qmT4, lambda j, g: qmT4[:, j:j + g, :], qm_buf,
                 lambda j: qm_ap(j * 64, (j + 1) * 64), 4, ident, rows=128, cols=64)
    psA = fw.next_ps()
    psB = fw.next_ps()
    for h in range(4):
        ps = psA if h < 2 else psB
        fw.op("pe", lambda e: e.matmul(ps[:, (h % 2) * 256:(h % 2 + 1) * 256], lhsT=qmT4[:, h, :],
                                       rhs=kmT4[:, h, :], start=True, stop=True),
              reads=[qmT4, kmT4], writes=[ps])
    for h2, ps in ((0, psA), (1, psB)):
        fw.op("dve", lambda e: e.tensor_reduce(out=st[:, h2 * 2:h2 * 2 + 2],
                                               in_=ps[:, :].rearrange("p (h m) -> p h m", h=2),
                                               axis=AX.X, op=ALU.max),
              reads=[ps], writes=[st])
    fw.op("dve", lambda e: e.tensor_scalar(out=st[:, 4:8], in0=st[:, 0:4], scalar1=-0.125, scalar2=None,
                                           op0=ALU.mult), reads=[st], writes=[st])
    for h in range(4):
        ps = psA if h < 2 else psB
        fw.op("act", lambda e: e.activation(out=p_sb[:, h, :], in_=ps[:, (h % 2) * 256:(h % 2 + 1) * 256],
                                            func=AF.Exp, bias=st[:, 4 + h:5 + h], scale=0.125,
                                            accum_out=st[:, 8 + h:9 + h]),
              reads=[ps, st], writes=[p_sb, st])
    fw.op("dve", lambda e: e.reciprocal(out=st[:, 12:16], in_=st[:, 8:12]), reads=[st], writes=[st])
    transpose_to(fw, pT, lambda j, g: pT[:, j:j + g, :], p_sb,
                 lambda j: p_sb[:, j // 2, (j % 2) * 128:(j % 2 + 1) * 128], 8, ident)
    ps = fw.next_ps()
    for h in range(4):
        for c in range(2):
            fw.op("pe", lambda e: e.matmul(ps[:, h * 64:(h + 1) * 64], lhsT=pT[:, h * 2 + c, :],
                                           rhs=vm[:, c, h * 64:(h + 1) * 64], start=(c == 0), stop=(c == 1)),
                  reads=[pT, vm], writes=[ps])
    for h in range(4):
        fw.op("dve", lambda e: e.tensor_scalar(out=om[:, om_off + h * 64:om_off + (h + 1) * 64],
                                               in0=ps[:, h * 64:(h + 1) * 64], scalar1=st[:, 12 + h:13 + h],
                                               scalar2=None, op0=ALU.mult),
              reads=[ps, st], writes=[om])


def out_proj_ln(fw, cst, om, omT, wo, xt, y, gam, bet, W):
    ident = cst[:, 0:128]
    transpose_to(fw, omT, lambda j, g: omT[:, j:j + g, :], om, lambda j: om[:, j * 128:(j + 1) * 128], 8, ident)
    pss = [fw.next_ps(), fw.next_ps()]
    for n in range(2):
        for k in range(8):
            fw.op("pe", lambda e: e.matmul(pss[n][:, :], lhsT=omT[:, k, :], rhs=wo[:, k, n * 512:(n + 1) * 512],
                                           start=(k == 0), stop=(k == 7)),
                  reads=[omT, wo], writes=[pss[n]])
    for n in range(2):
        fw.op("dve", lambda e: e.scalar_tensor_tensor(out=y[:, n * 512:(n + 1) * 512], in0=xt[:, n * 512:(n + 1) * 512],
                                                      scalar=ALPHA, in1=pss[n][:, :], op0=ALU.mult, op1=ALU.add),
              reads=[xt, pss[n]], writes=[y])
    layer_norm(fw, y, gam, bet, W)


def layer_norm(fw, y, gam, bet, W):
    bst, mv = W["bst"], W["mv"]
    for c in range(2):
        fw.op("dve", lambda e: e.bn_stats(out=bst[:, c, :], in_=y[:, c * 512:(c + 1) * 512]), reads=[y], writes=[bst])
    fw.op("dve", lambda e: e.bn_aggr(out=mv[:, 0:2], in_=bst[:, :, :]), reads=[bst], writes=[mv])
    fw.op("act", lambda e: e.activation(out=mv[:, 2:3], in_=mv[:, 1:2], func=AF.Ln, bias=EPS), reads=[mv], writes=[mv])
    fw.op("act", lambda e: e.activation(out=mv[:, 2:3], in_=mv[:, 2:3], func=AF.Exp, scale=-0.5), reads=[mv], writes=[mv])
    fw.op("dve", lambda e: e.tensor_scalar(out=y[:, :], in0=y[:, :], scalar1=mv[:, 0:1], scalar2=mv[:, 2:3],
                                           op0=ALU.subtract, op1=ALU.mult), reads=[y, mv], writes=[y])
    fw.op("pool", lambda e: e.tensor_tensor(out=y[:, :], in0=y[:, :], in1=gam[:, :], op=ALU.mult),
          reads=[y, gam], writes=[y])
    fw.op("pool", lambda e: e.tensor_tensor(out=y[:, :], in0=y[:, :], in1=bet[:, :], op=ALU.add),
          reads=[y, bet], writes=[y])


def phase_mixA(fw, nc, st, cst, x_d, mem_d, w_in_d, wg_d, hg_d, wkv_d, wo_d, lng_row, lnb_row,
               h_out_d, h_out_buf, ntiles):
    ident = cst[:, 0:128]
    tri_le = cst[:, 128:256]
    ones = cst[:, 384:512]
    w_in = fw.sbuf("a_w_in", [128, 8, 2576], BF16)
    for k in range(8):
        fw.dma(w_in[:, k, :], w_in_d[k * 128:(k + 1) * 128, :], writes=[w_in], q="pool")
    wo = fw.sbuf("a_wo", [128, 8, D], BF16)
    for k in range(8):
        fw.dma(wo[:, k, :], wo_d[k * 128:(k + 1) * 128, :], writes=[wo], q="pool")
    wg = fw.sbuf("a_wg", [32, 384], F32)
    fw.dma(wg[:], wg_d, writes=[wg])
    hg = load_bcast(fw, "a_hg", hg_d, 768)
    gam = load_bcast(fw, "a_gam", lng_row, D)
    bet = load_bcast(fw, "a_bet", lnb_row, D)
    kmT4, vm = mem_kv(fw, cst, mem_d, wkv_d, "a_")

    def mkset(p):
        sfx = "_%d" % p
        return dict(
            xt=fw.sbuf("xt" + sfx, [128, D], F32), xT=fw.sbuf("xT" + sfx, [128, 8, 128], BF16),
            proj=fw.sbuf("proj" + sfx, [128, 2576], F32), glrT=fw.sbuf("glrT" + sfx, [32, 128], F32),
            L=fw.sbuf("L" + sfx, [128, 384], F32), ex=fw.sbuf("ex" + sfx, [128, 3, 384], F32),
            qd=fw.sbuf("qd" + sfx, [128, 384], F32), ki=fw.sbuf("ki" + sfx, [128, 384], F32),
            kte=fw.sbuf("kte" + sfx, [128, 384], F32), qdT=fw.sbuf("qdT" + sfx, [64, 6, 128], F32),
            kiT=fw.sbuf("kiT" + sfx, [64, 6, 128], F32), dec=fw.sbuf("dec" + sfx, [64, 8], F32),
            attn=fw.sbuf("attn" + sfx, [128, 6, 128], F32), sq=fw.sbuf("sq" + sfx, [128, 6, 128], F32),
            ss=fw.sbuf("ss" + sfx, [128, 16], F32), sr=fw.sbuf("sr" + sfx, [128, 768], F32),
            om=fw.sbuf("om" + sfx, [128, D], F32), omT=fw.sbuf("omT" + sfx, [128, 8, 128], BF16),
            y=fw.sbuf("y" + sfx, [128, D], F32),
            W=dict(qmT4=fw.sbuf("qmT4" + sfx, [64, 4, 128], F32), p_sb=fw.sbuf("p_sb" + sfx, [128, 4, 256], F32),
                   pT=fw.sbuf("pT" + sfx, [128, 8, 128], F32), mst=fw.sbuf("mst" + sfx, [128, 16], F32),
                   bst=fw.sbuf("bst" + sfx, [128, 2, 6], F32), mv=fw.sbuf("mv" + sfx, [128, 4], F32)))

    sets = [mkset(0), mkset(1)]
    S = fw.sbuf("S", [64, 6, 128], F32)
    for p in range(2):
        fw.op("dve", lambda e: e.memset(sets[p]["glrT"][:], 1.0), writes=[sets[p]["glrT"]])
    fw.op("dve", lambda e: e.memset(S[:], 0.0), writes=[S])

    def tile_segments(ti):
        B = sets[ti % 2]
        xtt, xT, proj, glrT, L, ex = B["xt"], B["xT"], B["proj"], B["glrT"], B["L"], B["ex"]
        qd, ki, kte, qdT, kiT, dec = B["qd"], B["ki"], B["kte"], B["qdT"], B["kiT"], B["dec"]
        attn, sq, ss, sr, om, omT, yt, W = B["attn"], B["sq"], B["ss"], B["sr"], B["om"], B["omT"], B["y"], B["W"]
        q_ = proj[:, 0:384]
        k_ = proj[:, 384:768]
        r_ = proj[:, 1536:2304]
        st = {}

        def s_load():
            fw.dma(xtt[:], x_d[ti * 128:(ti + 1) * 128, :], writes=[xtt])

        def s_proj():
            transpose_to(fw, xT, lambda j, g: xT[:, j:j + g, :], xtt, lambda j: xtt[:, j * 128:(j + 1) * 128], 8, ident)
            for n in range(6):
                c0 = n * 512
                c1 = min(2576, c0 + 512)
                ps = fw.next_ps()
                for k in range(8):
                    fw.op("pe", lambda e: e.matmul(ps[:, 0:c1 - c0], lhsT=xT[:, k, :], rhs=w_in[:, k, c0:c1],
                                                   start=(k == 0), stop=(k == 7)),
                          reads=[xT, w_in], writes=[ps])
                if n % 2 == 0:
                    fw.op("act", lambda e: e.copy(out=proj[:, c0:c1], in_=ps[:, 0:c1 - c0]), reads=[ps], writes=[proj])
                else:
                    fw.op("dve", lambda e: e.tensor_copy(out=proj[:, c0:c1], in_=ps[:, 0:c1 - c0]), reads=[ps], writes=[proj])

        def s_gate():
            ps = fw.next_ps()
            fw.op("pe", lambda e: e.transpose(ps[0:16, 0:128], proj[:, 2304:2320], ident), reads=[proj], writes=[ps])
            fw.op("dve", lambda e: e.tensor_copy(out=glrT[0:16, :], in_=ps[0:16, 0:128]), reads=[ps], writes=[glrT])
            ps = fw.next_ps()
            fw.op("pe", lambda e: e.matmul(ps[:, 0:384], lhsT=glrT[:, :], rhs=wg[:, :], start=True, stop=True),
                  reads=[glrT, wg], writes=[ps])
            fw.op("act", lambda e: e.activation(out=L[:, :], in_=ps[:, 0:384], func=AF.Exp, scale=-1.0), reads=[ps], writes=[L])
            fw.op("act", lambda e: e.activation(out=L[:, :], in_=L[:, :], func=AF.Ln, bias=1.0), reads=[L], writes=[L])

        def s_cum():
            psc = fw.next_ps()
            fw.op("pe", lambda e: e.matmul(psc[:, 0:384], lhsT=tri_le, rhs=L[:, :], start=True, stop=True),
                  reads=[cst, L], writes=[psc])
            pst = fw.next_ps()
            fw.op("pe", lambda e: e.matmul(pst[:, 0:384], lhsT=ones, rhs=L[:, :], start=True, stop=True),
                  reads=[cst, L], writes=[pst])
            psd = fw.next_ps()
            for h in range(6):
                fw.op("pe", lambda e: e.matmul(psd[0:64, h * 2:h * 2 + 2], lhsT=L[:, h * 64:(h + 1) * 64], rhs=ones[:, 0:2],
                                               start=True, stop=True), reads=[cst, L], writes=[psd])
            fw.op("act", lambda e: e.activation(out=ex[:, 0, :], in_=psc[:, 0:384], func=AF.Exp, scale=-1.0 / 16), reads=[psc], writes=[ex])
            fw.op("act", lambda e: e.activation(out=ex[:, 1, :], in_=psc[:, 0:384], func=AF.Exp, scale=1.0 / 16), reads=[psc], writes=[ex])
            fw.op("act", lambda e: e.activation(out=ex[:, 2, :], in_=pst[:, 0:384], func=AF.Exp, scale=-1.0 / 16), reads=[pst], writes=[ex])
            fw.op("act", lambda e: e.activation(out=dec[:, 0:6], in_=psd[0:64, 0:12].rearrange("p (h two) -> p h two", two=2)[:, :, 0],
                                                func=AF.Exp, scale=-1.0 / 16), reads=[psd], writes=[dec])
            fw.op("dve", lambda e: e.scalar_tensor_tensor(out=qd[:, :], in0=q_, scalar=0.125, in1=ex[:, 0, :],
                                                          op0=ALU.mult, op1=ALU.mult), reads=[proj, ex], writes=[qd])
            fw.op("dve", lambda e: e.tensor_tensor(out=ki[:, :], in0=k_, in1=ex[:, 1, :], op=ALU.mult), reads=[proj, ex], writes=[ki])
            fw.op("dve", lambda e: e.tensor_tensor(out=kte[:, :], in0=ki[:, :], in1=ex[:, 2, :], op=ALU.mult), reads=[ki, ex], writes=[kte])

        def s_tr():
            transpose_to(fw, qdT, lambda j, g: qdT[:, j:j + g, :], qd, lambda j: qd[:, j * 64:(j + 1) * 64], 6, ident, rows=128, cols=64)
            transpose_to(fw, kiT, lambda j, g: kiT[:, j:j + g, :], ki, lambda j: ki[:, j * 64:(j + 1) * 64], 6, ident, rows=128, cols=64)

        def s_attn():
            pa = [fw.next_ps(), fw.next_ps()]
            for h in range(6):
                ps = pa[h // 4]
                fw.op("pe", lambda e: e.matmul(ps[:, (h % 4) * 128:(h % 4 + 1) * 128], lhsT=kiT[:, h, :], rhs=qdT[:, h, :],
                                               start=True, stop=True), reads=[kiT, qdT], writes=[ps])
            fw.op("dve", lambda e: e.tensor_tensor(out=attn[:, 0:4, :], in0=pa[0][:, :].rearrange("p (h c) -> p h c", h=4),
                                                   in1=tri_le.unsqueeze(1).broadcast_to([128, 4, 128]), op=ALU.mult),
                  reads=[pa[0], cst], writes=[attn])
            fw.op("dve", lambda e: e.tensor_tensor(out=attn[:, 4:6, :], in0=pa[1][:, 0:256].rearrange("p (h c) -> p h c", h=2),
                                                   in1=tri_le.unsqueeze(1).broadcast_to([128, 2, 128]), op=ALU.mult),
                  reads=[pa[1], cst], writes=[attn])

        def s_o():
            po = [fw.next_ps(), fw.next_ps()]
            st["po"] = po
            for h in range(6):
                ps = po[h // 4]
                oc = slice((h % 4) * 128, (h % 4 + 1) * 128)
                fw.op("pe", lambda e: e.matmul(ps[:, oc], lhsT=attn[:, h, :], rhs=proj[:, 768 + h * 128:768 + (h + 1) * 128],
                                               start=True, stop=False), reads=[attn, proj], writes=[ps])
                fw.op("pe", lambda e: e.matmul(ps[:, oc], lhsT=qdT[:, h, :], rhs=S[:, h, :], start=False, stop=True),
                      reads=[qdT, S], writes=[ps])
            for h in range(6):
                ps = fw.next_ps()
                fw.op("pe", lambda e: e.matmul(ps[0:64, 0:128], lhsT=kte[:, h * 64:(h + 1) * 64],
                                               rhs=proj[:, 768 + h * 128:768 + (h + 1) * 128], start=True, stop=True),
                      reads=[kte, proj], writes=[ps])
                fw.op("dve", lambda e: e.scalar_tensor_tensor(out=S[:, h, :], in0=S[:, h, :], scalar=dec[:, h:h + 1],
                                                              in1=ps[0:64, 0:128], op0=ALU.mult, op1=ALU.add),
                      reads=[S, dec, ps], writes=[S])
            for h in range(6):
                ps = po[h // 4]
                oc = slice((h % 4) * 128, (h % 4 + 1) * 128)
                fw.op("act", lambda e: e.activation(out=sq[:, h, :], in_=ps[:, oc], func=AF.Square, accum_out=ss[:, h:h + 1]),
                      reads=[ps], writes=[sq, ss])
            fw.op("act", lambda e: e.activation(out=ss[:, 8:14], in_=ss[:, 0:6], func=AF.Ln, scale=1.0 / 128, bias=EPS),
                  reads=[ss], writes=[ss])
            fw.op("act", lambda e: e.activation(out=ss[:, 8:14], in_=ss[:, 8:14], func=AF.Exp, scale=-0.5),
                  reads=[ss], writes=[ss])
            fw.op("act", lambda e: e.activation(out=sr[:, :], in_=r_, func=AF.Silu), reads=[proj], writes=[sr])
            for h in range(6):
                ps = po[h // 4]
                oc = slice((h % 4) * 128, (h % 4 + 1) * 128)
                fw.op("dve", lambda e: e.scalar_tensor_tensor(out=om[:, h * 128:(h + 1) * 128], in0=ps[:, oc],
                                                              scalar=ss[:, 8 + h:9 + h], in1=hg[:, h * 128:(h + 1) * 128],
                                                              op0=ALU.mult, op1=ALU.mult),
                      reads=[ps, ss, hg], writes=[om])
            fw.op("pool", lambda e: e.tensor_tensor(out=om[:, 0:768], in0=om[:, 0:768], in1=sr[:, :], op=ALU.mult),
                  reads=[om, sr], writes=[om])

        def s_mem():
            mem_attention(fw, cst, proj, lambda a, b: proj[:, 2320 + a:2320 + b], kmT4, vm, om, 768, W)

        def s_out():
            out_proj_ln(fw, cst, om, omT, wo, xtt, yt, gam, bet, W)
            fw.dma(h_out_d[ti * 128:(ti + 1) * 128, :], yt[:, :], reads=[yt])

        return [s_load, s_proj, s_gate, s_cum, s_tr, s_attn, s_o, s_mem, s_out]

    segs = [tile_segments(ti) for ti in range(ntiles)]
    NSEG = len(segs[0])
    SHIFT = 4
    for step in range((ntiles - 1) * SHIFT + NSEG):
        for ti in range(ntiles):
            k = step - ti * SHIFT
            if 0 <= k < NSEG:
                segs[ti][k]()


def phase_mixB(fw, nc, cst, h_d, mem_d, wkvsb_d, w_in_d, wkv_d, wo_d, lng_row, lnb_row, h_out_d, ntiles):
    ident = cst[:, 0:128]
    tri_lt = cst[:, 256:384]
    outer = fw.stack
    with ExitStack() as stB:
        fw.stack = stB
        KT = fw.sbuf("b_KT", [128, 6, T], BF16)
        Vt = fw.sbuf("b_Vt", [128, NT, 768], BF16)
        cb = fw.sbuf("b_cb", [128, 256], BF16)
        fw.op("dve", lambda e: e.tensor_copy(out=cb[:, 0:128], in_=cst[:, 512:640]), reads=[cst], writes=[cb])
        fw.op("dve", lambda e: e.tensor_copy(out=cb[:, 128:256], in_=cst[:, 384:512]), reads=[cst], writes=[cb])
        tri_ge_b = cb[:, 0:128]
        ones_b = cb[:, 128:256]
        with ExitStack() as s1:
            fw.stack = s1
            wkv = fw.sbuf("b_wkvsb", [128, 8, 1536], BF16)
            for k in range(8):
                fw.dma(wkv[:, k, :], wkvsb_d[k * 128:(k + 1) * 128, :], writes=[wkv], q="pool")
            ht = [fw.sbuf("b1_ht%d" % i, [128, D], F32) for i in range(2)]
            hT = fw.sbuf("b1_hT", [128, 8, 128], BF16)
            fw.dma(ht[0][:], h_d[0:128, :], writes=[ht[0]])
            for ti in range(ntiles):
                htt = ht[ti % 2]
                if ti + 1 < ntiles:
                    fw.dma(ht[(ti + 1) % 2][:], h_d[(ti + 1) * 128:(ti + 2) * 128, :], writes=[ht[(ti + 1) % 2]])
                transpose_to(fw, hT, lambda j, g: hT[:, j:j + g, :], htt, lambda j: htt[:, j * 128:(j + 1) * 128], 8, ident)
                for g in range(2):
                    ps = fw.next_ps()
                    for a in range(3):
                        pr = g * 3 + a
                        for k in range(8):
                            fw.op("pe", lambda e: e.matmul(ps[:, a * 128:(a + 1) * 128], lhsT=wkv[:, k, pr * 128:(pr + 1) * 128],
                                                           rhs=hT[:, k, :], start=(k == 0), stop=(k == 7)),
                                  reads=[wkv, hT], writes=[ps])
                    fw.op("act" if g == 0 else "dve",
                          (lambda e: e.copy(out=KT[:, g * 3:g * 3 + 3, ti * 128:(ti + 1) * 128],
                                            in_=ps[:, 0:384].rearrange("p (a n) -> p a n", a=3))) if g == 0 else
                          (lambda e: e.tensor_copy(out=KT[:, g * 3:g * 3 + 3, ti * 128:(ti + 1) * 128],
                                                   in_=ps[:, 0:384].rearrange("p (a n) -> p a n", a=3))),
                          reads=[ps], writes=[KT])
                for g in range(2):
                    ps = fw.next_ps()
                    for k in range(8):
                        fw.op("pe", lambda e: e.matmul(ps[:, 0:384], lhsT=hT[:, k, :], rhs=wkv[:, k, 768 + g * 384:768 + (g + 1) * 384],
                                                       start=(k == 0), stop=(k == 7)), reads=[wkv, hT], writes=[ps])
                    if g == 0:
                        fw.op("act", lambda e: e.copy(out=Vt[:, ti, 0:384], in_=ps[:, 0:384]), reads=[ps], writes=[Vt])
                    else:
                        fw.op("dve", lambda e: e.tensor_copy(out=Vt[:, ti, 384:768], in_=ps[:, 0:384]), reads=[ps], writes=[Vt])
            fw.barrier()
        fw.stack = stB
        kmT4, vm = mem_kv(fw, cst, mem_d, wkv_d, "b_")
        w_in = fw.sbuf("b_w_in", [128, 8, D], BF16)
        for k in range(8):
            fw.dma(w_in[:, k, :], w_in_d[k * 128:(k + 1) * 128, :], writes=[w_in], q="pool")
        wo = fw.sbuf("b_wo", [128, 8, D], BF16)
        for k in range(8):
            fw.dma(wo[:, k, :], wo_d[k * 128:(k + 1) * 128, :], writes=[wo], q="pool")
        gam = load_bcast(fw, "b_gam", lng_row, D)
        bet = load_bcast(fw, "b_bet", lnb_row, D)
        ht = [fw.sbuf("b_ht0", [128, D], F32)] * 2
        hT = fw.sbuf("b_hT", [128, 8, 128], BF16)
        QTm = fw.sbuf("b_QTm", [128, 12, 128], BF16)
        qm = fw.sbuf("b_qm", [128, 256], F32)
        Eb = [[fw.sbuf("b_E%d_%d" % (g, i), [128, 512], BF16) for i in range(2)] for g in range(3)]
        spb = [[fw.sbuf("b_sp%d_%d" % (g, i), [128, 512], BF16) for i in range(2)] for g in range(3)]
        Xb = [[fw.sbuf("b_X%d_%d" % (g, i), [128, 512], BF16) for i in range(2)] for g in range(3)]
        Ab = [[fw.sbuf("b_A%d_%d" % (g, i), [128, 512], BF16) for i in range(2)] for g in range(3)]
        spsum = [fw.sbuf("b_spsum%d" % g, [128, 512], BF16) for g in range(3)]
        om = fw.sbuf("b_om", [128, D], F32)
        omT = fw.sbuf("b_omT", [128, 8, 128], BF16)
        y = [fw.sbuf("b_y0", [128, D], F32)] * 2
        W = dict(qmT4=fw.sbuf("b_qmT4", [64, 4, 128], F32), p_sb=fw.sbuf("b_p_sb", [128, 4, 256], F32),
                 pT=fw.sbuf("b_pT", [128, 8, 128], F32), mst=fw.sbuf("b_mst", [128, 16], F32),
                 bst=fw.sbuf("b_bst", [128, 2, 6], F32), mv=fw.sbuf("b_mv", [128, 4], F32))
        fw.op("pool", lambda e: e.memset(QTm[:], 0.0), writes=[QTm])
        ACC = fw.PS[0:3]
        allps = fw.PS
        fw.PS = allps[3:8]
        fw.ps_i = 0
        u = 0
        for i in range(ntiles):
            htt = ht[0]
            yt = y[0]
            fw.dma(htt[:], h_d[i * 128:(i + 1) * 128, :], writes=[htt])
            transpose_to(fw, hT, lambda j, g: hT[:, j:j + g, :], htt, lambda j: htt[:, j * 128:(j + 1) * 128], 8, ident)
            for g in range(2):
                ps = fw.next_ps()
                for a in range(3):
                    pr = g * 3 + a
                    for k in range(8):
                        fw.op("pe", lambda e: e.matmul(ps[:, a * 128:(a + 1) * 128], lhsT=w_in[:, k, pr * 128:(pr + 1) * 128],
                                                       rhs=hT[:, k, :], start=(k == 0), stop=(k == 7)),
                              reads=[w_in, hT], writes=[ps])
                src = ps[:, 0:384].rearrange("p (a n) -> p a n", a=3)
                QTv = QTm[:, g * 6:(g + 1) * 6, :].rearrange("p (a e) n -> p a e n", e=2)
                fw.op("act", lambda e: e.copy(out=QTv[0:64, :, 0, :], in_=src[0:64, :, :]), reads=[ps], writes=[QTm])
                fw.op("dve", lambda e: e.tensor_copy(out=QTv[64:128, :, 1, :], in_=src[64:128, :, :]), reads=[ps], writes=[QTm])
            ps = fw.next_ps()
            for k in range(8):
                fw.op("pe", lambda e: e.matmul(ps[:, 0:256], lhsT=hT[:, k, :], rhs=w_in[:, k, 768:1024], start=(k == 0), stop=(k == 7)),
                      reads=[w_in, hT], writes=[ps])
            fw.op("act", lambda e: e.copy(out=qm[:, :], in_=ps[:, 0:256]), reads=[ps], writes=[qm])
            def stage_a(j, par):
                zps = [None] * 3
                for hg in range(3):
                    zps[hg] = fw.next_ps()
                    for pp in range(2):
                        pr = hg * 2 + pp
                        fw.op("pe", lambda e: e.matmul(zps[hg][:, pp * 256:(pp + 1) * 256], lhsT=KT[:, pr, j * 128:(j + 1) * 128],
                                                       rhs=QTm[:, 2 * pr:2 * pr + 2, :], start=True, stop=True),
                              reads=[KT, QTm], writes=[zps[hg]])
                for hg in range(3):
                    E = Eb[hg][par]
                    fw.op("act", lambda e: e.activation(out=E[:, :], in_=zps[hg][:, :], func=AF.Exp, scale=0.125), reads=[zps[hg]], writes=[E])
                    if j == i:
                        fw.op("dve", lambda e: e.tensor_tensor(out=E[:, :].rearrange("p (h t) -> p h t", h=4),
                                                               in0=E[:, :].rearrange("p (h t) -> p h t", h=4),
                                                               in1=tri_lt.unsqueeze(1).broadcast_to([128, 4, 128]), op=ALU.mult),
                              reads=[E, cst], writes=[E])
                for hg in range(3):
                    E, sp = Eb[hg][par], spb[hg][par]
                    fw.op("act", lambda e: e.activation(out=sp[:, :], in_=E[:, :], func=AF.Ln, bias=1.0), reads=[E], writes=[sp])

            def stage_b(j, par):
                cps = [None] * 3
                for hg in range(3):
                    sp = spb[hg][par]
                    cps[hg] = fw.next_ps()
                    fw.op("pe", lambda e: e.matmul(cps[hg][:, :], lhsT=tri_ge_b, rhs=sp[:, :], start=True, stop=(j == i)),
                          reads=[cb, sp], writes=[cps[hg]])
                    if j < i:
                        fw.op("pe", lambda e: e.matmul(cps[hg][:, :], lhsT=ones_b, rhs=spsum[hg][:, :], start=False, stop=True),
                              reads=[cb, spsum[hg]], writes=[cps[hg]])
                for hg in range(3):
                    X = Xb[hg][par]
                    fw.op("act", lambda e: e.activation(out=X[:, :], in_=cps[hg][:, :], func=AF.Exp, scale=-1.0), reads=[cps[hg]], writes=[X])
                for hg in range(3):
                    E, sp, X, A = Eb[hg][par], spb[hg][par], Xb[hg][par], Ab[hg][par]
                    fw.op("dve", lambda e: e.tensor_tensor(out=A[:, :], in0=E[:, :], in1=X[:, :], op=ALU.mult), reads=[E, X], writes=[A])
                    if j > 0:
                        if j == i:
                            fw.op("pool", lambda e: e.tensor_copy(out=spsum[hg][:, :], in_=sp[:, :]), reads=[sp], writes=[spsum[hg]])
                        else:
                            fw.op("pool", lambda e: e.tensor_tensor(out=spsum[hg][:, :], in0=spsum[hg][:, :], in1=sp[:, :], op=ALU.add),
                                  reads=[spsum[hg], sp], writes=[spsum[hg]])

            def stage_c(j, par):
                for hg in range(3):
                    A = Ab[hg][par]
                    for hh in range(4):
                        h = hg * 4 + hh
                        fw.op("pe", lambda e: e.matmul(ACC[hg][:, hh * 64:(hh + 1) * 64], lhsT=A[:, hh * 128:(hh + 1) * 128],
                                                       rhs=Vt[:, j, h * 64:(h + 1) * 64], start=(j == i and hh == 0), stop=(j == 0),
                                                       skip_group_check=True),
                              reads=[A, Vt], writes=[ACC[hg]])

            for j in range(i + 2, -1, -1):
                if 0 <= j - 2:
                    stage_a(j - 2, (j - 2) % 2)
                if 0 <= j - 1 <= i:
                    stage_b(j - 1, (j - 1) % 2)
                if j <= i:
                    stage_c(j, j % 2)
            for hg in range(3):
                fw.op("dve", lambda e: e.tensor_copy(out=om[:, hg * 256:(hg + 1) * 256], in_=ACC[hg][:, 0:256]), reads=[ACC[hg]], writes=[om])
            mem_attention(fw, cst, qm, lambda a, b: qm[:, a:b], kmT4, vm, om, 768, W)
            out_proj_ln(fw, cst, om, omT, wo, htt, yt, gam, bet, W)
            fw.dma(h_out_d[i * 128:(i + 1) * 128, :], yt[:, :], reads=[yt])
        fw.barrier()
        fw.PS = allps
        fw.ps_i = 0
    fw.stack = outer


RW = 384
U32 = mybir.dt.uint32


def peer_route(fw, nc, cst, h_d, wq_d, skT_d, rt_d, ntiles):
    ident = cst[:, 0:128]
    iota16 = cst[:, 640:656]
    with ExitStack() as st2:
        old = fw.stack
        fw.stack = st2
        wq = fw.sbuf("wq", [128, 8, 2048], BF16)
        for k in range(8):
            fw.dma(wq[:, k, :], wq_d[k * 128:(k + 1) * 128, :], writes=[wq], q="pool")
        skT = fw.sbuf("skT", [128, 16, 128], F32)
        fw.dma(skT[:], skT_d.rearrange("hp q k -> q hp k"), writes=[skT])
        ht = [fw.sbuf("r_ht%d" % i, [128, D], F32) for i in range(2)]
        hTs = [fw.sbuf("r_hT%d" % i, [128, 8, 128], BF16) for i in range(2)]
        qTs = [fw.sbuf("r_qT%d" % i, [128, 16, 128], F32) for i in range(2)]
        ssbs = [fw.sbuf("r_ssb%d" % i, [128, 16, 128], F32) for i in range(2)]
        rt = [fw.sbuf("r_rt%d" % i, [128, RW], F32) for i in range(2)]
        tv = fw.sbuf("r_tv", [128, 16, 16], F32)
        tiu = fw.sbuf("r_tiu", [128, 16, 16], U32)
        tif = fw.sbuf("r_tif", [128, 16, 16], F32)
        scr = fw.sbuf("r_scr", [128, 16, 128], F32)
        cand = fw.sbuf("r_cand", [128, 8, 256], F32)
        scr2 = fw.sbuf("r_scr2", [128, 8, 256], F32)
        ct = fw.sbuf("r_ct", [128, 8, 16], F32)
        ciu = fw.sbuf("r_ciu", [128, 8, 16], U32)
        jku = fw.sbuf("r_jku", [128, 2, 128], U32)
        jkf = fw.sbuf("r_jkf", [128, 2, 128], F32)
        ohs = [fw.sbuf("r_oh%d" % i, [128, 8, 16, 16], F32) for i in range(2)]
        sm = fw.sbuf("r_sm", [128, 8, 16], F32)
        zz = fw.sbuf("r_zz", [128, 16], F32)

        tvs = [Buf(tv.t, "tv%d" % i) for i in range(16)]
        tius = [Buf(tiu.t, "tiu%d" % i) for i in range(16)]
        scrs = [Buf(scr.t, "scr%d" % i) for i in range(16)]
        cts = [Buf(ct.t, "ct%d" % i) for i in range(8)]
        cius = [Buf(ciu.t, "ciu%d" % i) for i in range(8)]
        scr2s = [Buf(scr2.t, "scr2%d" % i) for i in range(8)]
        fw.dma(ht[0][:], h_d[0:128, :], writes=[ht[0]])
        for ti in range(ntiles):
            htt = ht[ti % 2]
            rtt = rt[ti % 2]
            hT, qT, ssb = hTs[ti % 2], qTs[ti % 2], ssbs[ti % 2]
            if ti + 1 < ntiles:
                fw.dma(ht[(ti + 1) % 2][:], h_d[(ti + 1) * 128:(ti + 2) * 128, :], writes=[ht[(ti + 1) % 2]])
            transpose_to(fw, hT, lambda j, g: hT[:, j:j + g, :], htt, lambda j: htt[:, j * 128:(j + 1) * 128], 8, ident)
            for g4 in range(4):
                ps = fw.next_ps()
                for a in range(4):
                    hp = g4 * 4 + a
                    for k in range(8):
                        fw.op("pe", lambda e: e.matmul(ps[:, a * 128:(a + 1) * 128], lhsT=wq[:, k, hp * 128:(hp + 1) * 128],
                                                       rhs=hT[:, k, :], start=(k == 0), stop=(k == 7)),
                              reads=[wq, hT], writes=[ps])
                if g4 % 2 == 0:
                    fw.op("act", lambda e: e.copy(out=qT[:, g4 * 4:g4 * 4 + 4, :], in_=ps[:, :].rearrange("p (a n) -> p a n", a=4)),
                          reads=[ps], writes=[qT])
                else:
                    fw.op("dve", lambda e: e.tensor_copy(out=qT[:, g4 * 4:g4 * 4 + 4, :], in_=ps[:, :].rearrange("p (a n) -> p a n", a=4)),
                          reads=[ps], writes=[qT])
            for g4 in range(4):
                ps = fw.next_ps()
                for a in range(4):
                    hp = g4 * 4 + a
                    fw.op("pe", lambda e: e.matmul(ps[:, a * 128:(a + 1) * 128], lhsT=qT[:, hp, :], rhs=skT[:, hp, :],
                                                   start=True, stop=True), reads=[qT, skT], writes=[ps])
                fw.op("act", lambda e: e.copy(out=ssb[:, g4 * 4:g4 * 4 + 4, :], in_=ps[:, :].rearrange("p (a k) -> p a k", a=4)),
                      reads=[ps], writes=[ssb])
            for hp in range(16):
                fw.op("dve", lambda e: e.max(out=tv[:, hp, 0:8], in_=ssb[:, hp, :]), reads=[ssb], writes=[tvs[hp]])
            for hp in range(16):
                fw.op("dve", lambda e: e.max_index(out=tiu[:, hp, 0:8], in_max=tv[:, hp, 0:8], in_values=ssb[:, hp, :]),
                      reads=[ssb, tvs[hp]], writes=[tius[hp]])
            for hp in range(16):
                fw.op("dve", lambda e: e.match_replace(out=scr[:, hp, :], in_to_replace=tv[:, hp, 0:8], in_values=ssb[:, hp, :],
                                                       imm_value=-1e30), reads=[ssb, tvs[hp]], writes=[scrs[hp]])
            for hp in range(16):
                fw.op("dve", lambda e: e.max(out=tv[:, hp, 8:16], in_=scr[:, hp, :]), reads=[scrs[hp]], writes=[tvs[hp]])
            for hp in range(16):
                fw.op("dve", lambda e: e.max_index(out=tiu[:, hp, 8:16], in_max=tv[:, hp, 8:16], in_values=scr[:, hp, :]),
                      reads=[scrs[hp], tvs[hp]], writes=[tius[hp]])
            fw.op("pool", lambda e: e.tensor_copy(out=tif[:, :, :], in_=tiu[:, :, :]), reads=tius, writes=[tif])
            tv4 = tv[:, :, :].rearrange("p (h t) j -> p h t j", t=2)
            a_ = tv4[:, :, 0, :]
            b_ = tv4[:, :, 1, :]
            tif4 = tif[:, :, :].rearrange("p (h t) j -> p h t j", t=2)
            fw.op("dve", lambda e: e.tensor_tensor(out=cand[:, :, :].rearrange("p h (j k) -> p h j k", j=16),
                                                   in0=a_.unsqueeze(3).broadcast_to([128, 8, 16, 16]),
                                                   in1=b_.unsqueeze(2).broadcast_to([128, 8, 16, 16]), op=ALU.add),
                  reads=tvs, writes=[cand])
            for h in range(8):
                fw.op("dve", lambda e: e.max(out=ct[:, h, 0:8], in_=cand[:, h, :]), reads=[cand], writes=[cts[h]])
            for h in range(8):
                fw.op("dve", lambda e: e.max_index(out=ciu[:, h, 0:8], in_max=ct[:, h, 0:8], in_values=cand[:, h, :]),
                      reads=[cand, cts[h]], writes=[cius[h]])
            for h in range(8):
                fw.op("dve", lambda e: e.match_replace(out=scr2[:, h, :], in_to_replace=ct[:, h, 0:8], in_values=cand[:, h, :],
                                                       imm_value=-1e30), reads=[cand, cts[h]], writes=[scr2s[h]])
            for h in range(8):
                fw.op("dve", lambda e: e.max(out=ct[:, h, 8:16], in_=scr2[:, h, :]), reads=[scr2s[h]], writes=[cts[h]])
            for h in range(8):
                fw.op("dve", lambda e: e.max_index(out=ciu[:, h, 8:16], in_max=ct[:, h, 8:16], in_values=scr2[:, h, :]),
                      reads=[scr2s[h], cts[h]], writes=[cius[h]])
            ci2 = ciu[:, :, :].rearrange("p h r -> p (h r)")
            fw.op("dve", lambda e: e.tensor_single_scalar(out=jku[:, 0, :], in_=ci2, scalar=4, op=ALU.logical_shift_right),
                  reads=cius, writes=[jku])
            fw.op("dve", lambda e: e.tensor_single_scalar(out=jku[:, 1, :], in_=ci2, scalar=15, op=ALU.bitwise_and),
                  reads=cius, writes=[jku])
            fw.op("pool", lambda e: e.tensor_copy(out=jkf[:, :, :], in_=jku[:, :, :]), reads=[jku], writes=[jkf])
            B4 = [128, 8, 16, 16]
            for t in range(2):
                sel = jkf[:, t, :].rearrange("p (h r) -> p h r", h=8)
                oht = ohs[t]
                fw.op("dve", lambda e: e.tensor_tensor(out=oht[:, :, :, :], in0=sel.unsqueeze(3).broadcast_to(B4),
                                                       in1=iota16.unsqueeze(1).unsqueeze(1).broadcast_to(B4), op=ALU.is_equal),
                      reads=[jkf, cst], writes=[oht])
                fw.op("pool", lambda e: e.tensor_tensor(out=oht[:, :, :, :], in0=oht[:, :, :, :],
                                                        in1=tif4[:, :, t, :].unsqueeze(2).broadcast_to(B4), op=ALU.mult),
                      reads=[oht, tif], writes=[oht])
                fw.op("dve", lambda e: e.tensor_reduce(out=rtt[:, t * 128:(t + 1) * 128].rearrange("p (h r) -> p h r", h=8),
                                                       in_=oht[:, :, :, :], axis=AX.X, op=ALU.add),
                      reads=[oht], writes=[rtt])
            fw.op("dve", lambda e: e.tensor_tensor(out=sm[:, :, :], in0=ct[:, :, :], in1=ct[:, :, 0:1].broadcast_to([128, 8, 16]),
                                                   op=ALU.subtract), reads=cts, writes=[sm])
            fw.op("act", lambda e: e.activation(out=sm[:, :, :], in_=sm[:, :, :], func=AF.Exp), reads=[sm], writes=[sm])
            fw.op("dve", lambda e: e.tensor_reduce(out=zz[:, 0:8], in_=sm[:, :, :], axis=AX.X, op=ALU.add), reads=[sm], writes=[zz])
            fw.op("dve", lambda e: e.reciprocal(out=zz[:, 8:16], in_=zz[:, 0:8]), reads=[zz], writes=[zz])
            fw.op("dve", lambda e: e.tensor_tensor(out=rtt[:, 256:384].rearrange("p (h r) -> p h r", h=8), in0=sm[:, :, :],
                                                   in1=zz[:, 8:16].unsqueeze(2).broadcast_to([128, 8, 16]), op=ALU.mult),
                  reads=[sm, zz], writes=[rtt])
            fw.dma(rt_d[ti * 128:(ti + 1) * 128, :], rtt[:, :], reads=[rtt])
        fw.barrier()
        fw.stack = old


def precast_tables(fw, uh_d, vh_d, ub_d, vb_d):
    for c in range(128):
        fw.dma(ub_d[c], uh_d[c], q="pool")
        fw.dma(vb_d[c], vh_d[c], q="pool")


def peer_dense(fw, nc, cst, h_d, rt_d, ub_d, vb_d, lng_row, lnb_row, out_d, ntiles, NTB=2):
    ident = cst[:, 0:128]
    iota_i = cst[:, 640:768]
    TB = NTB * 128
    NS = 16
    with ExitStack() as st2:
        old = fw.stack
        fw.stack = st2
        gam = load_bcast(fw, "d_gam", lng_row, D)
        bet = load_bcast(fw, "d_bet", lnb_row, D)
        hprep = fw.sbuf("d_hprep", [128, D], F32)
        hT = [fw.sbuf("d_hT%d" % i, [128, 8, TB], BF16) for i in range(2)]
        rt = [fw.sbuf("d_rt%d" % i, [128, RW], F32) for i in range(2)]
        IT = [fw.sbuf("d_IT%d" % i, [128, 3, 128], F32) for i in range(2)]
        PTg = [fw.sbuf("d_PT%d" % i, [128, NS, 128], BF16) for i in range(2)]
        QTg = [fw.sbuf("d_QT%d" % i, [128, NS, 128], BF16) for i in range(2)]
        Cb = [fw.sbuf("d_Cb%d" % i, [128, TB * 128], BF16) for i in range(2)]
        NB = 4
        Ub = [fw.sbuf("d_U%d" % i, [128, 8, 128], BF16) for i in range(NB)]
        Vb = [fw.sbuf("d_V%d" % i, [128, D], BF16) for i in range(NB)]
        SK = 2
        NR = SK + 1
        G = [fw.sbuf("d_G%d" % i, [128, TB], BF16) for i in range(NR)]
        CA = [fw.sbuf("d_CA%d" % i, [128, TB], BF16) for i in range(NR)]
        ys = [fw.sbuf("d_y%d" % i, [128, D], F32) for i in range(2)]
        W = dict(bst=fw.sbuf("d_bst", [128, 2, 6], F32), mv=fw.sbuf("d_mv", [128, 4], F32))
        allps = fw.PS
        ACC = allps[0:4]
        fw.PS = allps[4:8]
        fw.ps_i = 0
        nblk = ntiles // NTB
        NG = 128 // NS
        NSTEP = NTB * NG

        def cbuild_steps(b):
            par = b % 2
            Cb3 = Cb[par][:, :].rearrange("p (n i) -> p n i", i=128)
            steps = []

            def dma_step(t):
                def f():
                    ti = b * NTB + t
                    fw.dma(hprep[:], h_d[ti * 128:(ti + 1) * 128, :], writes=[hprep])
                    fw.dma(rt[t % 2][:], rt_d[ti * 128:(ti + 1) * 128, :], writes=[rt[t % 2]])
                return f

            def tr_step(t):
                def f():
                    transpose_to(fw, hT[par], lambda j, g: hT[par][:, j:j + g, t * 128:(t + 1) * 128], hprep,
                                 lambda j: hprep[:, j * 128:(j + 1) * 128], 8, ident, evac=("act",))
                    transpose_to(fw, IT[t % 2], lambda j, g: IT[t % 2][:, j:j + g, :], rt[t % 2],
                                 lambda j: rt[t % 2][:, j * 128:(j + 1) * 128], 3, ident, evac=("act",))
                return f

            def build_step(q):
                def f():
                    t, g = q // NG, q % NG
                    it = IT[t % 2]
                    n0 = g * NS
                    Bs = [128, NS, 128]
                    fw.op("dve", lambda e: e.tensor_tensor(out=PTg[q % 2][:, :, :], in0=iota_i.unsqueeze(1).broadcast_to(Bs),
                                                           in1=it[:, 0, n0:n0 + NS].unsqueeze(2).broadcast_to(Bs), op=ALU.is_equal),
                          reads=[cst, it], writes=[PTg[q % 2]])
                    fw.op("pool", lambda e: e.tensor_tensor(out=PTg[q % 2][:, :, :], in0=PTg[q % 2][:, :, :],
                                                            in1=it[:, 2, n0:n0 + NS].unsqueeze(2).broadcast_to(Bs), op=ALU.mult),
                          reads=[PTg[q % 2], it], writes=[PTg[q % 2]])
                    fw.op("dve", lambda e: e.tensor_tensor(out=QTg[q % 2][:, :, :], in0=iota_i.unsqueeze(1).broadcast_to(Bs),
                                                           in1=it[:, 1, n0:n0 + NS].unsqueeze(2).broadcast_to(Bs), op=ALU.is_equal),
                          reads=[cst, it], writes=[QTg[q % 2]])
                return f

            def mm_step(q):
                def f():
                    t, g = q // NG, q % NG
                    for g4 in range(NS // 4):
                        ps = fw.next_ps()
                        for a in range(4):
                            n = g4 * 4 + a
                            fw.op("pe", lambda e: e.matmul(ps[:, a * 128:(a + 1) * 128], lhsT=PTg[q % 2][:, n, :], rhs=QTg[q % 2][:, n, :],
                                                           start=True, stop=True), reads=[PTg[q % 2], QTg[q % 2]], writes=[ps])
                        nb = t * 128 + g * NS + g4 * 4
                        fw.op("act", lambda e: e.copy(out=Cb3[:, nb:nb + 4, :], in_=ps[:, :].rearrange("p (a i) -> p a i", a=4)),
                              reads=[ps], writes=[Cb[par]])
                return f

            sched = {}
            pos = 0
            for t in range(NTB):
                sched.setdefault(pos, []).append(dma_step(t))
                sched.setdefault(pos + 1, []).append(tr_step(t))
                for g in range(NG):
                    q = t * NG + g
                    sched.setdefault(pos + 2 + g, []).append(build_step(q))
                    sched.setdefault(pos + 3 + g, []).append(mm_step(q))
                pos += NG
            nsteps = max(sched) + 1
            for p in range(nsteps):
                fl = sched.get(p, [])
                steps.append(lambda fl=fl: [f() for f in fl])
            return steps

        def load_chunk(c):
            fw.dma(Ub[c % NB][:], ub_d[c].rearrange("p (k i) -> p k i", k=8), writes=[Ub[c % NB]])
            fw.dma(Vb[c % NB][:], vb_d[c], writes=[Vb[c % NB]])

        for st_ in cbuild_steps(0):
            st_()
        for bi in range(nblk):
            par = bi % 2
            Cbc = Cb[par][:, :].rearrange("p (n i) -> p i n", i=128)
            nxt = cbuild_steps(bi + 1) if bi + 1 < nblk else []
            every = max(1, 120 // max(1, len(nxt))) if nxt else 0
            for c in range(NB):
                load_chunk(c)

            def front(c):
                U = Ub[c % NB]
                ps = fw.next_ps()
                for k in range(8):
                    fw.op("pe", lambda e: e.matmul(ps[:, 0:TB], lhsT=U[:, k, :], rhs=hT[par][:, k, :], start=(k == 0), stop=(k == 7)),
                          reads=[U, hT[par]], writes=[ps])
                fw.op("act", lambda e: e.activation(out=G[c % NR][:, :], in_=ps[:, 0:TB], func=AF.Gelu), reads=[ps], writes=[G[c % NR]])
                fw.op("dve", lambda e: e.tensor_tensor(out=CA[c % NR][:, :], in0=G[c % NR][:, :], in1=Cbc[:, c, :], op=ALU.mult),
                      reads=[G[c % NR], Cb[par]], writes=[CA[c % NR]])

            for c in range(SK):
                front(c)
            si = 0
            for c in range(128):
                if c + SK < 128:
                    front(c + SK)
                V = Vb[c % NB]
                ca = CA[c % NR]
                for t in range(NTB):
                    for hf in range(2):
                        acc = ACC[t * 2 + hf]
                        fw.op("pe", lambda e: e.matmul(acc[:, :], lhsT=ca[:, t * 128:(t + 1) * 128], rhs=V[:, hf * 512:(hf + 1) * 512],
                                                       start=(c == 0), stop=(c == 127)), reads=[ca, V], writes=[acc])
                if c + NB < 128:
                    load_chunk(c + NB)
                if nxt and c % every == every - 1 and si < len(nxt):
                    nxt[si]()
                    si += 1
            while si < len(nxt):
                nxt[si]()
                si += 1
            for t in range(NTB):
                ti = bi * NTB + t
                fw.dma(ys[t % 2][:], h_d[ti * 128:(ti + 1) * 128, :], writes=[ys[t % 2]])
            for t in range(NTB):
                ti = bi * NTB + t
                y = ys[t % 2]
                for hf in range(2):
                    acc = ACC[t * 2 + hf]
                    fw.op("dve", lambda e: e.scalar_tensor_tensor(out=y[:, hf * 512:(hf + 1) * 512], in0=y[:, hf * 512:(hf + 1) * 512],
                                                                  scalar=ALPHA, in1=acc[:, :], op0=ALU.mult, op1=ALU.add),
                          reads=[y, acc], writes=[y])
                layer_norm(fw, y, gam, bet, W)
                fw.dma(out_d[ti * 128:(ti + 1) * 128, :], y[:, :], reads=[y], q="pool")
        fw.barrier()
        fw.PS = allps
        fw.ps_i = 0
        fw.stack = old


_CACHE = {}
_DBG = {}


def kernel(**inputs):
    n = 8
    stop_after = int(inputs.pop("_stop_after", 4))
    start_at = int(inputs.pop("_start_at", 1))
    ntiles = int(inputs.pop("_ntiles", NT))
    ntb = int(inputs.pop("_ntb", 2))
    cores = inputs.pop("_cores", list(range(n)))
    key = (stop_after, ntiles, start_at, ntb)
    if key not in _CACHE:
        _CACHE[key] = build_program(stop_after, ntiles, start_at, ntb)
    nc = _CACHE[key]
    f = lambda a: np.ascontiguousarray(np.asarray(a, dtype=np.float32))
    wg = np.zeros((32, 384), np.float32)
    wg[0:16] = inputs["a_w_gate2"][0]
    wg[16] = inputs["a_b_gate"][0]
    shared = {
        "a_w_in": f(inputs["a_w_in"][0]),
        "a_wg": wg,
        "a_hg": f(inputs["a_head_g"][0]).reshape(1, 768),
        "a_wkv": f(inputs["a_w_mem_kv"][0]),
        "a_wo": f(inputs["a_w_out"][0]),
        "ln_g": f(inputs["ln_g"]).reshape(4, D),
        "ln_b": f(inputs["ln_b"]).reshape(4, D),
        "consts": make_consts(),
        "sbkv": f(inputs["sb_w_kv"]),
        "b_w_in": f(inputs["b_w_in"][0]),
        "b_wkv": f(inputs["b_w_mem_kv"][0]),
        "b_wo": f(inputs["b_w_out"][0]),
    }
    for l in range(2):
        shared["wq%d" % l] = f(inputs["peer_w_q"][l])
        shared["skT%d" % l] = f(np.asarray(inputs["peer_subkeys"][l]).reshape(16, 128, 128).transpose(0, 2, 1))
        u = np.asarray(inputs["peer_u"][l], dtype=np.float32).reshape(128, 128, 8, 128)
        shared["uh%d" % l] = np.ascontiguousarray(u.transpose(1, 3, 2, 0)).reshape(128, 128, D)
        v = np.asarray(inputs["peer_v"][l], dtype=np.float32).reshape(128, 128, D)
        shared["vh%d" % l] = np.ascontiguousarray(v.transpose(1, 0, 2))
    in_maps = []
    for b in cores:
        m = dict(shared)
        m["x"] = f(inputs["x"][b])
        m["mem"] = f(inputs["mem"][b])
        in_maps.append(m)
    res = run_bass_kernel_spmd(nc, in_maps, core_ids=list(range(len(cores))))
    return np.stack([r["out"] for r in res.results], axis=0)
```

```python
import numpy as np
from contextlib import ExitStack
import concourse.bass as bass
import concourse.mybir as mybir
from concourse.bass_utils import run_bass_kernel_spmd

F32 = mybir.dt.float32
BF16 = mybir.dt.bfloat16
AF = mybir.ActivationFunctionType
ALU = mybir.AluOpType
AX = mybir.AxisListType

T = 4096
D = 1024
NT = T // 128
ALPHA = (2.0 * 2) ** 0.25
EPS = 1e-5


class Buf:
    __slots__ = ("t", "name", "excl", "lw", "rd")

    def __init__(self, t, name, excl=False):
        self.t = t
        self.name = name
        self.excl = excl
        self.lw = None
        self.rd = []

    def __getitem__(self, k):
        return self.t[k]


class Eng:
    def __init__(self, name, h):
        self.name = name
        self.h = h
        self.sem = None
        self.cnt = 0
        self.waited = {}


class FW:
    def __init__(self, nc, stack, n_dma_sems=32):
        self.nc = nc
        self.stack = stack
        self.E = {}
        for name, h in (("pe", nc.tensor), ("act", nc.scalar), ("dve", nc.vector),
                        ("pool", nc.gpsimd), ("sp", nc.sync)):
            e = Eng(name, h)
            e.sem = stack.enter_context(nc.semaphore("s_" + name))
            self.E[name] = e
        self.dma_sems = [stack.enter_context(nc.semaphore("d%d" % i)) for i in range(n_dma_sems)]
        self.dma_tot = [0] * n_dma_sems
        self.dma_ring = {"sp": list(range(0, n_dma_sems - 8)), "pool": list(range(n_dma_sems - 8, n_dma_sems))}
        self.dma_pos = {"sp": 0, "pool": 0}
        self.n_ins = 0
        self.ps_i = 0
        self.PS = []

    def sbuf(self, name, shape, dt):
        self.n_alloc = getattr(self, "n_alloc", 0) + 1
        t = self.stack.enter_context(self.nc.sbuf_tensor("sb%d_%s" % (self.n_alloc, name), list(shape), dt))
        return Buf(t, name)

    def psum(self, name, shape, dt):
        t = self.stack.enter_context(self.nc.psum_tensor(name, list(shape), dt))
        return Buf(t, name, excl=True)

    def next_ps(self):
        b = self.PS[self.ps_i]
        self.ps_i = (self.ps_i + 1) % len(self.PS)
        return b

    def _need(self, eng, deps, skip_same, out):
        best = {}
        for (k, v) in deps:
            if skip_same and k is eng.sem:
                continue
            if v > best.get(k, 0):
                best[k] = v
        for k, v in best.items():
            if eng.waited.get(k, 0) < v and out.get(k, 0) < v:
                out[k] = v

    def _wait(self, eng, deps, skip_same):
        need = {}
        self._need(eng, deps, skip_same, need)
        for k, v in need.items():
            eng.h.wait_ge(k, v)
            eng.waited[k] = v

    def _sync(self, e, reads, writes, is_pe):
        raw = []
        other = []
        for b in reads:
            if b.lw is not None:
                raw.append(b.lw)
            if b.excl:
                other.extend(b.rd)
        for b in writes:
            if b.lw is not None:
                other.append(b.lw)
            other.extend(b.rd)
        need = {}
        self._need(e, raw, is_pe, need)
        self._need(e, other, is_pe, need)
        items = list(need.items())
        for k, v in items[:-1]:
            e.h.wait_ge(k, v)
            e.waited[k] = v
        if items:
            k, v = items[-1]
            e.waited[k] = v
            return (k, v)
        return None

    def _mark(self, ev, reads, writes):
        for b in reads:
            if b.excl:
                b.lw = ev
                b.rd = []
            else:
                b.rd.append(ev)
                if len(b.rd) > 48:
                    b.rd = b.rd[-48:]
        for b in writes:
            b.lw = ev
            b.rd = []

    def op(self, eng, fn, reads=(), writes=()):
        e = self.E[eng]
        w = self._sync(e, reads, writes, eng == "pe")
        ins = fn(e.h)
        if w is not None:
            ins._wait_ge(w[0], w[1])
        e.cnt += 1
        ins.then_inc(e.sem, 1)
        self._mark((e.sem, e.cnt), reads, writes)
        self.n_ins += 1
        return ins

    def dma(self, out_ap, in_ap, reads=(), writes=(), q="sp", **kw):
        e = self.E[q]
        ring = self.dma_ring[q]
        i = ring[self.dma_pos[q]]
        self.dma_pos[q] = (self.dma_pos[q] + 1) % len(ring)
        sem = self.dma_sems[i]
        w = self._sync(e, reads, writes, False)
        if w is not None:
            e.h.wait_ge(w[0], w[1])
        if self.dma_tot[i] > 0:
            self._wait(e, [(sem, self.dma_tot[i])], False)
        ins = e.h.dma_start(out=out_ap, in_=in_ap, **kw)
        self.dma_tot[i] += 16
        ins.then_inc(sem, 16)
        self._mark((sem, self.dma_tot[i]), reads, writes)
        self.n_ins += 1
        return ins

    def barrier(self):
        evs = [(e.sem, e.cnt) for e in self.E.values() if e.cnt > 0]
        evs += [(s, t) for s, t in zip(self.dma_sems, self.dma_tot) if t > 0]
        for e in self.E.values():
            self._wait(e, evs, skip_same=True)


def make_consts():
    c = np.zeros((128, 768), np.float32)
    i = np.arange(128)
    c[:, 0:128] = np.eye(128, dtype=np.float32)
    c[:, 128:256] = (i[:, None] <= i[None, :]).astype(np.float32)
    c[:, 256:384] = (i[:, None] < i[None, :]).astype(np.float32)
    c[:, 384:512] = 1.0
    c[:, 512:640] = (i[:, None] >= i[None, :]).astype(np.float32)
    c[:, 640:768] = i[None, :].astype(np.float32)
    return c


def build_program(stop_after=4, ntiles=NT, start_at=1, ntb=1):
    nc = bass.Bass("TRN2", target_bir_lowering=False)
    dt_in = lambda name, shape: nc.dram_tensor(name, list(shape), F32, kind="ExternalInput").ap()
    x_d = dt_in("x", [T, D])
    mem_d = dt_in("mem", [256, D])
    a_w_in_d = dt_in("a_w_in", [D, 2576])
    a_wg_d = dt_in("a_wg", [32, 384])
    a_hg_d = dt_in("a_hg", [1, 768])
    a_wkv_d = dt_in("a_wkv", [D, 512])
    a_wo_d = dt_in("a_wo", [D, D])
    ln_g_d = dt_in("ln_g", [4, D])
    ln_b_d = dt_in("ln_b", [4, D])
    consts_d = dt_in("consts", [128, 768])
    sbkv_d = dt_in("sbkv", [D, 1536])
    b_w_in_d = dt_in("b_w_in", [D, D])
    b_wkv_d = dt_in("b_wkv", [D, 512])
    b_wo_d = dt_in("b_wo", [D, D])
    wq_d = [dt_in("wq%d" % l, [D, 2048]) for l in range(2)]
    skT_d = [dt_in("skT%d" % l, [16, 128, 128]) for l in range(2)]
    uh_d = [dt_in("uh%d" % l, [128, 128, D]) for l in range(2)]
    vh_d = [dt_in("vh%d" % l, [128, 128, D]) for l in range(2)]
    out_d = nc.dram_tensor("out", [T, D], F32, kind="ExternalOutput").ap()
    scr = lambda name, shape, dt: nc.dram_tensor(name, list(shape), dt, kind="Internal").ap()
    H = [None, scr("H1", [T, D], F32), scr("H2", [T, D], F32), scr("H3", [T, D], F32), None]
    rt_d = scr("RT", [T, RW], F32)
    ub_d = [scr("ub%d" % l, [128, 128, D], BF16) for l in range(2)]
    vb_d = [scr("vb%d" % l, [128, 128, D], BF16) for l in range(2)]
    H[start_at - 1] = x_d
    H[stop_after] = out_d

    with ExitStack() as st:
        fw = FW(nc, st)
        for i in range(8):
            fw.PS.append(fw.psum("ps%d" % i, [128, 512], F32))
        cst = fw.sbuf("cst", [128, 768], F32)
        fw.dma(cst[:], consts_d, writes=[cst])
        fw.barrier()
        if start_at <= 2 <= stop_after:
            precast_tables(fw, uh_d[0], vh_d[0], ub_d[0], vb_d[0])
        if start_at == 4:
            precast_tables(fw, uh_d[1], vh_d[1], ub_d[1], vb_d[1])
        for ph in range(start_at, stop_after + 1):
            if ph == 3 and stop_after >= 4:
                precast_tables(fw, uh_d[1], vh_d[1], ub_d[1], vb_d[1])
            if ph == 1:
                with ExitStack() as st1:
                    fw.stack = st1
                    phase_mixA(fw, nc, st1, cst, H[0], mem_d, a_w_in_d, a_wg_d, a_hg_d, a_wkv_d, a_wo_d,
                               ln_g_d[0:1, :], ln_b_d[0:1, :], H[1], None, ntiles)
                    fw.barrier()
                fw.stack = st
            elif ph in (2, 4):
                l = (ph - 2) // 2
                peer_route(fw, nc, cst, H[ph - 1], wq_d[l], skT_d[l], rt_d, ntiles)
                if not _DBG.get("route_only"):
                    peer_dense(fw, nc, cst, H[ph - 1], rt_d, ub_d[l], vb_d[l], ln_g_d[2 * l + 1:2 * l + 2, :],
                               ln_b_d[2 * l + 1:2 * l + 2, :], H[ph], ntiles, NTB=ntb)
            elif ph == 3:
                phase_mixB(fw, nc, cst, H[2], mem_d, sbkv_d, b_w_in_d, b_wkv_d, b_wo_d, ln_g_d[2:3, :], ln_b_d[2:3, :], H[3], ntiles)
        fw.barrier()
        print("n_ins", fw.n_ins)
    return nc


def load_bcast(fw, name, src_row, n):
    t = fw.sbuf(name, [128, n], F32)
    fw.dma(t[:], src_row.partition_broadcast(128), writes=[t])
    return t


def transpose_to(fw, dst_buf, dst_ap_fn, src_buf, src_ap_fn, n, ident, rows=128, cols=128,
                 evac=("act", "dve")):
    j = 0
    gi = 0
    while j < n:
        g = min(4, n - j)
        ps = fw.next_ps()
        for a in range(g):
            fw.op("pe", lambda e: e.transpose(ps[0:cols, a * 128:a * 128 + rows], src_ap_fn(j + a),
                                              ident[0:rows, 0:rows]),
                  reads=[src_buf], writes=[ps])
        eng = evac[gi % len(evac)]
        src = ps[0:cols, 0:g * 128].rearrange("p (a r) -> p a r", a=g)[:, :, 0:rows]
        if eng == "act":
            fw.op("act", lambda e: e.copy(out=dst_ap_fn(j, g), in_=src), reads=[ps], writes=[dst_buf])
        else:
            fw.op(eng, lambda e: e.tensor_copy(out=dst_ap_fn(j, g), in_=src), reads=[ps], writes=[dst_buf])
        j += g
        gi += 1


def mem_kv(fw, cst, mem_d, wkv_d, pfx):
    ident = cst[:, 0:128]
    kmT4 = fw.sbuf(pfx + "kmT4", [64, 4, 256], F32)
    vm = fw.sbuf(pfx + "vm", [128, 2, 256], F32)
    outer = fw.stack
    with ExitStack() as tmp:
        fw.stack = tmp
        memt = fw.sbuf(pfx + "memt", [128, 2, D], F32)
        fw.dma(memt[:], mem_d.rearrange("(c p) d -> p c d", p=128), writes=[memt])
        wkv = fw.sbuf(pfx + "wkv", [128, 8, 512], F32)
        fw.dma(wkv[:], wkv_d.rearrange("(k p) n -> p k n", p=128), writes=[wkv])
        memT = fw.sbuf(pfx + "memT", [128, 8, 256], F32)
        for c in range(2):
            transpose_to(fw, memT, lambda j, g: memT[:, j:j + g, c * 128:(c + 1) * 128],
                         memt, lambda j: memt[:, c, j * 128:(j + 1) * 128], 8, ident)
        for h in range(4):
            ps = fw.next_ps()
            for k in range(8):
                fw.op("pe", lambda e: e.matmul(ps[0:64, 0:256], lhsT=wkv[:, k, h * 64:(h + 1) * 64],
                                               rhs=memT[:, k, :], start=(k == 0), stop=(k == 7)),
                      reads=[wkv, memT], writes=[ps])
            fw.op("dve", lambda e: e.tensor_copy(out=kmT4[:, h, :], in_=ps[0:64, 0:256]), reads=[ps], writes=[kmT4])
        for c in range(2):
            ps = fw.next_ps()
            for k in range(8):
                fw.op("pe", lambda e: e.matmul(ps[:, 0:256], lhsT=memT[:, k, c * 128:(c + 1) * 128],
                                               rhs=wkv[:, k, 256:512], start=(k == 0), stop=(k == 7)),
                      reads=[wkv, memT], writes=[ps])
            fw.op("dve", lambda e: e.tensor_copy(out=vm[:, c, :], in_=ps[:, 0:256]), reads=[ps], writes=[vm])
        fw.barrier()
    fw.stack = outer
    return kmT4, vm


def mem_attention(fw, cst, qm_buf, qm_ap, kmT4, vm, om, om_off, W):
    ident = cst[:, 0:128]
    qmT4, p_sb, pT, st = W["qmT4"], W["p_sb"], W["pT"], W["mst"]
    transpose_to(fw, qmT4, lambda j, g: qmT4[:, j:j + g, :], qm_buf,
                 lambda j: qm_ap(j * 64, (j + 1) * 64), 4, ident, rows=128, cols=64)
    psA = fw.next_ps()
    psB = fw.next_ps()
    for h in range(4):
        ps = psA if h < 2 else psB
        fw.op("pe", lambda e: e.matmul(ps[:, (h % 2) * 256:(h % 2 + 1) * 256], lhsT=qmT4[:, h, :],
                                       rhs=kmT4[:, h, :], start=True, stop=True),
              reads=[qmT4, kmT4], writes=[ps])
    for h2, ps in ((0, psA), (1, psB)):
        fw.op("dve", lambda e: e.tensor_reduce(out=st[:, h2 * 2:h2 * 2 + 2],
                                               in_=ps[:, :].rearrange("p (h m) -> p h m", h=2),
                                               axis=AX.X, op=ALU.max),
              reads=[ps], writes=[st])
    fw.op("dve", lambda e: e.tensor_scalar(out=st[:, 4:8], in0=st[:, 0:4], scalar1=-0.125, scalar2=None,
                                           op0=ALU.mult), reads=[st], writes=[st])
    for h in range(4):
        ps = psA if h < 2 else psB
        fw.op("act", lambda e: e.activation(out=p_sb[:, h, :], in_=ps[:, (h % 2) * 256:(h % 2 + 1) * 256],
                                            func=AF.Exp, bias=st[:, 4 + h:5 + h], scale=0.125,
                                            accum_out=st[:, 8 + h:9 + h]),
              reads=[ps, st], writes=[p_sb, st])
    fw.op("dve", lambda e: e.reciprocal(out=st[:, 12:16], in_=st[:, 8:12]), reads=[st], writes=[st])
    transpose_to(fw, pT, lambda j, g: pT[:, j:j + g, :], p_sb,
                 lambda j: p_sb[:, j // 2, (j % 2) * 128:(j % 2 + 1) * 128], 8, ident)
    ps = fw.next_ps()
    for h in range(4):
        for c in range(2):
            fw.op("pe", lambda e: e.matmul(ps[:, h * 64:(h + 1) * 64], lhsT=pT[:, h * 2 + c, :],
                                           rhs=vm[:, c, h * 64:(h + 1) * 64], start=(c == 0), stop=(c == 1)),
                  reads=[pT, vm], writes=[ps])
    for h in range(4):
        fw.op("dve", lambda e: e.tensor_scalar(out=om[:, om_off + h * 64:om_off + (h + 1) * 64],
                                               in0=ps[:, h * 64:(h + 1) * 64], scalar1=st[:, 12 + h:13 + h],
                                               scalar2=None, op0=ALU.mult),
              reads=[ps, st], writes=[om])


def out_proj_ln(fw, cst, om, omT, wo, xt, y, gam, bet, W):
    ident = cst[:, 0:128]
    transpose_to(fw, omT, lambda j, g: omT[:, j:j + g, :], om, lambda j: om[:, j * 128:(j + 1) * 128], 8, ident)
    pss = [fw.next_ps(), fw.next_ps()]
    for n in range(2):
        for k in range(8):
            fw.op("pe", lambda e: e.matmul(pss[n][:, :], lhsT=omT[:, k, :], rhs=wo[:, k, n * 512:(n + 1) * 512],
                                           start=(k == 0), stop=(k == 7)),
                  reads=[omT, wo], writes=[pss[n]])
    for n in range(2):
        fw.op("dve", lambda e: e.scalar_tensor_tensor(out=y[:, n * 512:(n + 1) * 512], in0=xt[:, n * 512:(n + 1) * 512],
                                                      scalar=ALPHA, in1=pss[n][:, :], op0=ALU.mult, op1=ALU.add),
              reads=[xt, pss[n]], writes=[y])
    layer_norm(fw, y, gam, bet, W)


def layer_norm(fw, y, gam, bet, W):
    bst, mv = W["bst"], W["mv"]
    for c in range(2):
        fw.op("dve", lambda e: e.bn_stats(out=bst[:, c, :], in_=y[:, c * 512:(c + 1) * 512]), reads=[y], writes=[bst])
    fw.op("dve", lambda e: e.bn_aggr(out=mv[:, 0:2], in_=bst[:, :, :]), reads=[bst], writes=[mv])
    fw.op("act", lambda e: e.activation(out=mv[:, 2:3], in_=mv[:, 1:2], func=AF.Ln, bias=EPS), reads=[mv], writes=[mv])
    fw.op("act", lambda e: e.activation(out=mv[:, 2:3], in_=mv[:, 2:3], func=AF.Exp, scale=-0.5), reads=[mv], writes=[mv])
    fw.op("dve", lambda e: e.tensor_scalar(out=y[:, :], in0=y[:, :], scalar1=mv[:, 0:1], scalar2=mv[:, 2:3],
                                           op0=ALU.subtract, op1=ALU.mult), reads=[y, mv], writes=[y])
    fw.op("pool", lambda e: e.tensor_tensor(out=y[:, :], in0=y[:, :], in1=gam[:, :], op=ALU.mult),
          reads=[y, gam], writes=[y])
    fw.op("pool", lambda e: e.tensor_tensor(out=y[:, :], in0=y[:, :], in1=bet[:, :], op=ALU.add),
          reads=[y, bet], writes=[y])


def phase_mixA(fw, nc, st, cst, x_d, mem_d, w_in_d, wg_d, hg_d, wkv_d, wo_d, lng_row, lnb_row,
               h_out_d, h_out_buf, ntiles):
    ident = cst[:, 0:128]
    tri_le = cst[:, 128:256]
    ones = cst[:, 384:512]
    w_in = fw.sbuf("a_w_in", [128, 8, 2576], BF16)
    for k in range(8):
        fw.dma(w_in[:, k, :], w_in_d[k * 128:(k + 1) * 128, :], writes=[w_in], q="pool")
    wo = fw.sbuf("a_wo", [128, 8, D], BF16)
    for k in range(8):
        fw.dma(wo[:, k, :], wo_d[k * 128:(k + 1) * 128, :], writes=[wo], q="pool")
    wg = fw.sbuf("a_wg", [32, 384], F32)
    fw.dma(wg[:], wg_d, writes=[wg])
    hg = load_bcast(fw, "a_hg", hg_d, 768)
    gam = load_bcast(fw, "a_gam", lng_row, D)
    bet = load_bcast(fw, "a_bet", lnb_row, D)
    kmT4, vm = mem_kv(fw, cst, mem_d, wkv_d, "a_")

    def mkset(p):
        sfx = "_%d" % p
        return dict(
            xt=fw.sbuf("xt" + sfx, [128, D], F32), xT=fw.sbuf("xT" + sfx, [128, 8, 128], BF16),
            proj=fw.sbuf("proj" + sfx, [128, 2576], F32), glrT=fw.sbuf("glrT" + sfx, [32, 128], F32),
            L=fw.sbuf("L" + sfx, [128, 384], F32), ex=fw.sbuf("ex" + sfx, [128, 3, 384], F32),
            qd=fw.sbuf("qd" + sfx, [128, 384], F32), ki=fw.sbuf("ki" + sfx, [128, 384], F32),
            kte=fw.sbuf("kte" + sfx, [128, 384], F32), qdT=fw.sbuf("qdT" + sfx, [64, 6, 128], F32),
            kiT=fw.sbuf("kiT" + sfx, [64, 6, 128], F32), dec=fw.sbuf("dec" + sfx, [64, 8], F32),
            attn=fw.sbuf("attn" + sfx, [128, 6, 128], F32), sq=fw.sbuf("sq" + sfx, [128, 6, 128], F32),
            ss=fw.sbuf("ss" + sfx, [128, 16], F32), sr=fw.sbuf("sr" + sfx, [128, 768], F32),
            om=fw.sbuf("om" + sfx, [128, D], F32), omT=fw.sbuf("omT" + sfx, [128, 8, 128], BF16),
            y=fw.sbuf("y" + sfx, [128, D], F32),
            W=dict(qmT4=fw.sbuf("qmT4" + sfx, [64, 4, 128], F32), p_sb=fw.sbuf("p_sb" + sfx, [128, 4, 256], F32),
                   pT=fw.sbuf("pT" + sfx, [128, 8, 128], F32), mst=fw.sbuf("mst" + sfx, [128, 16], F32),
                   bst=fw.sbuf("bst" + sfx, [128, 2, 6], F32), mv=fw.sbuf("mv" + sfx, [128, 4], F32)))

    sets = [mkset(0), mkset(1)]
    S = fw.sbuf("S", [64, 6, 128], F32)
    for p in range(2):
        fw.op("dve", lambda e: e.memset(sets[p]["glrT"][:], 1.0), writes=[sets[p]["glrT"]])
    fw.op("dve", lambda e: e.memset(S[:], 0.0), writes=[S])

    def tile_segments(ti):
        B = sets[ti % 2]
        xtt, xT, proj, glrT, L, ex = B["xt"], B["xT"], B["proj"], B["glrT"], B["L"], B["ex"]
        qd, ki, kte, qdT, kiT, dec = B["qd"], B["ki"], B["kte"], B["qdT"], B["kiT"], B["dec"]
        attn, sq, ss, sr, om, omT, yt, W = B["attn"], B["sq"], B["ss"], B["sr"], B["om"], B["omT"], B["y"], B["W"]
        q_ = proj[:, 0:384]
        k_ = proj[:, 384:768]
        r_ = proj[:, 1536:2304]
        st = {}

        def s_load():
            fw.dma(xtt[:], x_d[ti * 128:(ti + 1) * 128, :], writes=[xtt])

        def s_proj():
            transpose_to(fw, xT, lambda j, g: xT[:, j:j + g, :], xtt, lambda j: xtt[:, j * 128:(j + 1) * 128], 8, ident)
            for n in range(6):
                c0 = n * 512
                c1 = min(2576, c0 + 512)
                ps = fw.next_ps()
                for k in range(8):
                    fw.op("pe", lambda e: e.matmul(ps[:, 0:c1 - c0], lhsT=xT[:, k, :], rhs=w_in[:, k, c0:c1],
                                                   start=(k == 0), stop=(k == 7)),
                          reads=[xT, w_in], writes=[ps])
                if n % 2 == 0:
                    fw.op("act", lambda e: e.copy(out=proj[:, c0:c1], in_=ps[:, 0:c1 - c0]), reads=[ps], writes=[proj])
                else:
                    fw.op("dve", lambda e: e.tensor_copy(out=proj[:, c0:c1], in_=ps[:, 0:c1 - c0]), reads=[ps], writes=[proj])

        def s_gate():
            ps = fw.next_ps()
            fw.op("pe", lambda e: e.transpose(ps[0:16, 0:128], proj[:, 2304:2320], ident), reads=[proj], writes=[ps])
            fw.op("dve", lambda e: e.tensor_copy(out=glrT[0:16, :], in_=ps[0:16, 0:128]), reads=[ps], writes=[glrT])
            ps = fw.next_ps()
            fw.op("pe", lambda e: e.matmul(ps[:, 0:384], lhsT=glrT[:, :], rhs=wg[:, :], start=True, stop=True),
                  reads=[glrT, wg], writes=[ps])
            fw.op("act", lambda e: e.activation(out=L[:, :], in_=ps[:, 0:384], func=AF.Exp, scale=-1.0), reads=[ps], writes=[L])
            fw.op("act", lambda e: e.activation(out=L[:, :], in_=L[:, :], func=AF.Ln, bias=1.0), reads=[L], writes=[L])

        def s_cum():
            psc = fw.next_ps()
            fw.op("pe", lambda e: e.matmul(psc[:, 0:384], lhsT=tri_le, rhs=L[:, :], start=True, stop=True),
                  reads=[cst, L], writes=[psc])
            pst = fw.next_ps()
            fw.op("pe", lambda e: e.matmul(pst[:, 0:384], lhsT=ones, rhs=L[:, :], start=True, stop=True),
                  reads=[cst, L], writes=[pst])
            psd = fw.next_ps()
            for h in range(6):
                fw.op("pe", lambda e: e.matmul(psd[0:64, h * 2:h * 2 + 2], lhsT=L[:, h * 64:(h + 1) * 64], rhs=ones[:, 0:2],
                                               start=True, stop=True), reads=[cst, L], writes=[psd])
            fw.op("act", lambda e: e.activation(out=ex[:, 0, :], in_=psc[:, 0:384], func=AF.Exp, scale=-1.0 / 16), reads=[psc], writes=[ex])
            fw.op("act", lambda e: e.activation(out=ex[:, 1, :], in_=psc[:, 0:384], func=AF.Exp, scale=1.0 / 16), reads=[psc], writes=[ex])
            fw.op("act", lambda e: e.activation(out=ex[:, 2, :], in_=pst[:, 0:384], func=AF.Exp, scale=-1.0 / 16), reads=[pst], writes=[ex])
            fw.op("act", lambda e: e.activation(out=dec[:, 0:6], in_=psd[0:64, 0:12].rearrange("p (h two) -> p h two", two=2)[:, :, 0],
                                                func=AF.Exp, scale=-1.0 / 16), reads=[psd], writes=[dec])
            fw.op("dve", lambda e: e.scalar_tensor_tensor(out=qd[:, :], in0=q_, scalar=0.125, in1=ex[:, 0, :],
                                                          op0=ALU.mult, op1=ALU.mult), reads=[proj, ex], writes=[qd])
            fw.op("dve", lambda e: e.tensor_tensor(out=ki[:, :], in0=k_, in1=ex[:, 1, :], op=ALU.mult), reads=[proj, ex], writes=[ki])
            fw.op("dve", lambda e: e.tensor_tensor(out=kte[:, :], in0=ki[:, :], in1=ex[:, 2, :], op=ALU.mult), reads=[ki, ex], writes=[kte])

        def s_tr():
            transpose_to(fw, qdT, lambda j, g: qdT[:, j:j + g, :], qd, lambda j: qd[:, j * 64:(j + 1) * 64], 6, ident, rows=128, cols=64)
            transpose_to(fw, kiT, lambda j, g: kiT[:, j:j + g, :], ki, lambda j: ki[:, j * 64:(j + 1) * 64], 6, ident, rows=128, cols=64)

        def s_attn():
            pa = [fw.next_ps(), fw.next_ps()]
            for h in range(6):
                ps = pa[h // 4]
                fw.op("pe", lambda e: e.matmul(ps[:, (h % 4) * 128:(h % 4 + 1) * 128], lhsT=kiT[:, h, :], rhs=qdT[:, h, :],
                                               start=True, stop=True), reads=[kiT, qdT], writes=[ps])
            fw.op("dve", lambda e: e.tensor_tensor(out=attn[:, 0:4, :], in0=pa[0][:, :].rearrange("p (h c) -> p h c", h=4),
                                                   in1=tri_le.unsqueeze(1).broadcast_to([128, 4, 128]), op=ALU.mult),
                  reads=[pa[0], cst], writes=[attn])
            fw.op("dve", lambda e: e.tensor_tensor(out=attn[:, 4:6, :], in0=pa[1][:, 0:256].rearrange("p (h c) -> p h c", h=2),
                                                   in1=tri_le.unsqueeze(1).broadcast_to([128, 2, 128]), op=ALU.mult),
                  reads=[pa[1], cst], writes=[attn])

        def s_o():
            po = [fw.next_ps(), fw.next_ps()]
            st["po"] = po
            for h in range(6):
                ps = po[h // 4]
                oc = slice((h % 4) * 128, (h % 4 + 1) * 128)
                fw.op("pe", lambda e: e.matmul(ps[:, oc], lhsT=attn[:, h, :], rhs=proj[:, 768 + h * 128:768 + (h + 1) * 128],
                                               start=True, stop=False), reads=[attn, proj], writes=[ps])
                fw.op("pe", lambda e: e.matmul(ps[:, oc], lhsT=qdT[:, h, :], rhs=S[:, h, :], start=False, stop=True),
                      reads=[qdT, S], writes=[ps])
            for h in range(6):
                ps = fw.next_ps()
                fw.op("pe", lambda e: e.matmul(ps[0:64, 0:128], lhsT=kte[:, h * 64:(h + 1) * 64],
                                               rhs=proj[:, 768 + h * 128:768 + (h + 1) * 128], start=True, stop=True),
                      reads=[kte, proj], writes=[ps])
                fw.op("dve", lambda e: e.scalar_tensor_tensor(out=S[:, h, :], in0=S[:, h, :], scalar=dec[:, h:h + 1],
                                                              in1=ps[0:64, 0:128], op0=ALU.mult, op1=ALU.add),
                      reads=[S, dec, ps], writes=[S])
            for h in range(6):
                ps = po[h // 4]
                oc = slice((h % 4) * 128, (h % 4 + 1) * 128)
                fw.op("act", lambda e: e.activation(out=sq[:, h, :], in_=ps[:, oc], func=AF.Square, accum_out=ss[:, h:h + 1]),
                      reads=[ps], writes=[sq, ss])
            fw.op("act", lambda e: e.activation(out=ss[:, 8:14], in_=ss[:, 0:6], func=AF.Ln, scale=1.0 / 128, bias=EPS),
                  reads=[ss], writes=[ss])
            fw.op("act", lambda e: e.activation(out=ss[:, 8:14], in_=ss[:, 8:14], func=AF.Exp, scale=-0.5),
                  reads=[ss], writes=[ss])
            fw.op("act", lambda e: e.activation(out=sr[:, :], in_=r_, func=AF.Silu), reads=[proj], writes=[sr])
            for h in range(6):
                ps = po[h // 4]
                oc = slice((h % 4) * 128, (h % 4 + 1) * 128)
                fw.op("dve", lambda e: e.scalar_tensor_tensor(out=om[:, h * 128:(h + 1) * 128], in0=ps[:, oc],
                                                              scalar=ss[:, 8 + h:9 + h], in1=hg[:, h * 128:(h + 1) * 128],
                                                              op0=ALU.mult, op1=ALU.mult),
                      reads=[ps, ss, hg], writes=[om])
            fw.op("pool", lambda e: e.tensor_tensor(out=om[:, 0:768], in0=om[:, 0:768], in1=sr[:, :], op=ALU.mult),
                  reads=[om, sr], writes=[om])

        def s_mem():
            mem_attention(fw, cst, proj, lambda a, b: proj[:, 2320 + a:2320 + b], kmT4, vm, om, 768, W)

        def s_out():
            out_proj_ln(fw, cst, om, omT, wo, xtt, yt, gam, bet, W)
            fw.dma(h_out_d[ti * 128:(ti + 1) * 128, :], yt[:, :], reads=[yt])

        return [s_load, s_proj, s_gate, s_cum, s_tr, s_attn, s_o, s_mem, s_out]

    segs = [tile_segments(ti) for ti in range(ntiles)]
    NSEG = len(segs[0])
    SHIFT = 4
    for step in range((ntiles - 1) * SHIFT + NSEG):
        for ti in range(ntiles):
            k = step - ti * SHIFT
            if 0 <= k < NSEG:
                segs[ti][k]()


def phase_mixB(fw, nc, cst, h_d, mem_d, wkvsb_d, w_in_d, wkv_d, wo_d, lng_row, lnb_row, h_out_d, ntiles):
    ident = cst[:, 0:128]
    tri_lt = cst[:, 256:384]
    outer = fw.stack
    with ExitStack() as stB:
        fw.stack = stB
        KT = fw.sbuf("b_KT", [128, 6, T], BF16)
        Vt = fw.sbuf("b_Vt", [128, NT, 768], BF16)
        cb = fw.sbuf("b_cb", [128, 256], BF16)
        fw.op("dve", lambda e: e.tensor_copy(out=cb[:, 0:128], in_=cst[:, 512:640]), reads=[cst], writes=[cb])
        fw.op("dve", lambda e: e.tensor_copy(out=cb[:, 128:256], in_=cst[:, 384:512]), reads=[cst], writes=[cb])
        tri_ge_b = cb[:, 0:128]
        ones_b = cb[:, 128:256]
        with ExitStack() as s1:
            fw.stack = s1
            wkv = fw.sbuf("b_wkvsb", [128, 8, 1536], BF16)
            for k in range(8):
                fw.dma(wkv[:, k, :], wkvsb_d[k * 128:(k + 1) * 128, :], writes=[wkv], q="pool")
            ht = [fw.sbuf("b1_ht%d" % i, [128, D], F32) for i in range(2)]
            hT = fw.sbuf("b1_hT", [128, 8, 128], BF16)
            fw.dma(ht[0][:], h_d[0:128, :], writes=[ht[0]])
            for ti in range(ntiles):
                htt = ht[ti % 2]
                if ti + 1 < ntiles:
                    fw.dma(ht[(ti + 1) % 2][:], h_d[(ti + 1) * 128:(ti + 2) * 128, :], writes=[ht[(ti + 1) % 2]])
                transpose_to(fw, hT, lambda j, g: hT[:, j:j + g, :], htt, lambda j: htt[:, j * 128:(j + 1) * 128], 8, ident)
                for g in range(2):
                    ps = fw.next_ps()
                    for a in range(3):
                        pr = g * 3 + a
                        for k in range(8):
                            fw.op("pe", lambda e: e.matmul(ps[:, a * 128:(a + 1) * 128], lhsT=wkv[:, k, pr * 128:(pr + 1) * 128],
                                                           rhs=hT[:, k, :], start=(k == 0), stop=(k == 7)),
                                  reads=[wkv, hT], writes=[ps])
                    fw.op("act" if g == 0 else "dve",
                          (lambda e: e.copy(out=KT[:, g * 3:g * 3 + 3, ti * 128:(ti + 1) * 128],
                                            in_=ps[:, 0:384].rearrange("p (a n) -> p a n", a=3))) if g == 0 else
                          (lambda e: e.tensor_copy(out=KT[:, g * 3:g * 3 + 3, ti * 128:(ti + 1) * 128],
                                                   in_=ps[:, 0:384].rearrange("p (a n) -> p a n", a=3))),
                          reads=[ps], writes=[KT])
                for g in range(2):
                    ps = fw.next_ps()
                    for k in range(8):
                        fw.op("pe", lambda e: e.matmul(ps[:, 0:384], lhsT=hT[:, k, :], rhs=wkv[:, k, 768 + g * 384:768 + (g + 1) * 384],
                                                       start=(k == 0), stop=(k == 7)), reads=[wkv, hT], writes=[ps])
                    if g == 0:
                        fw.op("act", lambda e: e.copy(out=Vt[:, ti, 0:384], in_=ps[:, 0:384]), reads=[ps], writes=[Vt])
                    else:
                        fw.op("dve", lambda e: e.tensor_copy(out=Vt[:, ti, 384:768], in_=ps[:, 0:384]), reads=[ps], writes=[Vt])
            fw.barrier()
        fw.stack = stB
        kmT4, vm = mem_kv(fw, cst, mem_d, wkv_d, "b_")
        w_in = fw.sbuf("b_w_in", [128, 8, D], BF16)
        for k in range(8):
            fw.dma(w_in[:, k, :], w_in_d[k * 128:(k + 1) * 128, :], writes=[w_in], q="pool")
        wo = fw.sbuf("b_wo", [128, 8, D], BF16)
        for k in range(8):
            fw.dma(wo[:, k, :], wo_d[k * 128:(k + 1) * 128, :], writes=[wo], q="pool")
        gam = load_bcast(fw, "b_gam", lng_row, D)
        bet = load_bcast(fw, "b_bet", lnb_row, D)
        ht = [fw.sbuf("b_ht0", [128, D], F32)] * 2
        hT = fw.sbuf("b_hT", [128, 8, 128], BF16)
        QTm = fw.sbuf("b_QTm", [128, 12, 128], BF16)
        qm = fw.sbuf("b_qm", [128, 256], F32)
        Eb = [[fw.sbuf("b_E%d_%d" % (g, i), [128, 512], BF16) for i in range(2)] for g in range(3)]
        spb = [[fw.sbuf("b_sp%d_%d" % (g, i), [128, 512], BF16) for i in range(2)] for g in range(3)]
        Xb = [[fw.sbuf("b_X%d_%d" % (g, i), [128, 512], BF16) for i in range(2)] for g in range(3)]
        Ab = [[fw.sbuf("b_A%d_%d" % (g, i), [128, 512], BF16) for i in range(2)] for g in range(3)]
        spsum = [fw.sbuf("b_spsum%d" % g, [128, 512], BF16) for g in range(3)]
        om = fw.sbuf("b_om", [128, D], F32)
        omT = fw.sbuf("b_omT", [128, 8, 128], BF16)
        y = [fw.sbuf("b_y0", [128, D], F32)] * 2
        W = dict(qmT4=fw.sbuf("b_qmT4", [64, 4, 128], F32), p_sb=fw.sbuf("b_p_sb", [128, 4, 256], F32),
                 pT=fw.sbuf("b_pT", [128, 8, 128], F32), mst=fw.sbuf("b_mst", [128, 16], F32),
                 bst=fw.sbuf("b_bst", [128, 2, 6], F32), mv=fw.sbuf("b_mv", [128, 4], F32))
        fw.op("pool", lambda e: e.memset(QTm[:], 0.0), writes=[QTm])
        ACC = fw.PS[0:3]
        allps = fw.PS
        fw.PS = allps[3:8]
        fw.ps_i = 0
        u = 0
        for i in range(ntiles):
            htt = ht[0]
            yt = y[0]
            fw.dma(htt[:], h_d[i * 128:(i + 1) * 128, :], writes=[htt])
            transpose_to(fw, hT, lambda j, g: hT[:, j:j + g, :], htt, lambda j: htt[:, j * 128:(j + 1) * 128], 8, ident)
            for g in range(2):
                ps = fw.next_ps()
                for a in range(3):
                    pr = g * 3 + a
                    for k in range(8):
                        fw.op("pe", lambda e: e.matmul(ps[:, a * 128:(a + 1) * 128], lhsT=w_in[:, k, pr * 128:(pr + 1) * 128],
                                                       rhs=hT[:, k, :], start=(k == 0), stop=(k == 7)),
                              reads=[w_in, hT], writes=[ps])
                src = ps[:, 0:384].rearrange("p (a n) -> p a n", a=3)
                QTv = QTm[:, g * 6:(g + 1) * 6, :].rearrange("p (a e) n -> p a e n", e=2)
                fw.op("act", lambda e: e.copy(out=QTv[0:64, :, 0, :], in_=src[0:64, :, :]), reads=[ps], writes=[QTm])
                fw.op("dve", lambda e: e.tensor_copy(out=QTv[64:128, :, 1, :], in_=src[64:128, :, :]), reads=[ps], writes=[QTm])
            ps = fw.next_ps()
            for k in range(8):
                fw.op("pe", lambda e: e.matmul(ps[:, 0:256], lhsT=hT[:, k, :], rhs=w_in[:, k, 768:1024], start=(k == 0), stop=(k == 7)),
                      reads=[w_in, hT], writes=[ps])
            fw.op("act", lambda e: e.copy(out=qm[:, :], in_=ps[:, 0:256]), reads=[ps], writes=[qm])
            def stage_a(j, par):
                zps = [None] * 3
                for hg in range(3):
                    zps[hg] = fw.next_ps()
                    for pp in range(2):
                        pr = hg * 2 + pp
                        fw.op("pe", lambda e: e.matmul(zps[hg][:, pp * 256:(pp + 1) * 256], lhsT=KT[:, pr, j * 128:(j + 1) * 128],
                                                       rhs=QTm[:, 2 * pr:2 * pr + 2, :], start=True, stop=True),
                              reads=[KT, QTm], writes=[zps[hg]])
                for hg in range(3):
                    E = Eb[hg][par]
                    fw.op("act", lambda e: e.activation(out=E[:, :], in_=zps[hg][:, :], func=AF.Exp, scale=0.125), reads=[zps[hg]], writes=[E])
                    if j == i:
                        fw.op("dve", lambda e: e.tensor_tensor(out=E[:, :].rearrange("p (h t) -> p h t", h=4),
                                                               in0=E[:, :].rearrange("p (h t) -> p h t", h=4),
                                                               in1=tri_lt.unsqueeze(1).broadcast_to([128, 4, 128]), op=ALU.mult),
                              reads=[E, cst], writes=[E])
                for hg in range(3):
                    E, sp = Eb[hg][par], spb[hg][par]
                    fw.op("act", lambda e: e.activation(out=sp[:, :], in_=E[:, :], func=AF.Ln, bias=1.0), reads=[E], writes=[sp])

            def stage_b(j, par):
                cps = [None] * 3
                for hg in range(3):
                    sp = spb[hg][par]
                    cps[hg] = fw.next_ps()
                    fw.op("pe", lambda e: e.matmul(cps[hg][:, :], lhsT=tri_ge_b, rhs=sp[:, :], start=True, stop=(j == i)),
                          reads=[cb, sp], writes=[cps[hg]])
                    if j < i:
                        fw.op("pe", lambda e: e.matmul(cps[hg][:, :], lhsT=ones_b, rhs=spsum[hg][:, :], start=False, stop=True),
                              reads=[cb, spsum[hg]], writes=[cps[hg]])
                for hg in range(3):
                    X = Xb[hg][par]
                    fw.op("act", lambda e: e.activation(out=X[:, :], in_=cps[hg][:, :], func=AF.Exp, scale=-1.0), reads=[cps[hg]], writes=[X])
                for hg in range(3):
                    E, sp, X, A = Eb[hg][par], spb[hg][par], Xb[hg][par], Ab[hg][par]
                    fw.op("dve", lambda e: e.tensor_tensor(out=A[:, :], in0=E[:, :], in1=X[:, :], op=ALU.mult), reads=[E, X], writes=[A])
                    if j > 0:
                        if j == i:
                            fw.op("pool", lambda e: e.tensor_copy(out=spsum[hg][:, :], in_=sp[:, :]), reads=[sp], writes=[spsum[hg]])
                        else:
                            fw.op("pool", lambda e: e.tensor_tensor(out=spsum[hg][:, :], in0=spsum[hg][:, :], in1=sp[:, :], op=ALU.add),
                                  reads=[spsum[hg], sp], writes=[spsum[hg]])

            def stage_c(j, par):
                for hg in range(3):
                    A = Ab[hg][par]
                    for hh in range(4):
                        h = hg * 4 + hh
                        fw.op("pe", lambda e: e.matmul(ACC[hg][:, hh * 64:(hh + 1) * 64], lhsT=A[:, hh * 128:(hh + 1) * 128],
                                                       rhs=Vt[:, j, h * 64:(h + 1) * 64], start=(j == i and hh == 0), stop=(j == 0),
                                                       skip_group_check=True),
                              reads=[A, Vt], writes=[ACC[hg]])

            for j in range(i + 2, -1, -1):
                if 0 <= j - 2:
                    stage_a(j - 2, (j - 2) % 2)
                if 0 <= j - 1 <= i:
                    stage_b(j - 1, (j - 1) % 2)
                if j <= i:
                    stage_c(j, j % 2)
            for hg in range(3):
                fw.op("dve", lambda e: e.tensor_copy(out=om[:, hg * 256:(hg + 1) * 256], in_=ACC[hg][:, 0:256]), reads=[ACC[hg]], writes=[om])
            mem_attention(fw, cst, qm, lambda a, b: qm[:, a:b], kmT4, vm, om, 768, W)
            out_proj_ln(fw, cst, om, omT, wo, htt, yt, gam, bet, W)
            fw.dma(h_out_d[i * 128:(i + 1) * 128, :], yt[:, :], reads=[yt])
        fw.barrier()
        fw.PS = allps
        fw.ps_i = 0
    fw.stack = outer


RW = 384
U32 = mybir.dt.uint32


def peer_route(fw, nc, cst, h_d, wq_d, skT_d, rt_d, ntiles):
    ident = cst[:, 0:128]
    iota16 = cst[:, 640:656]
    with ExitStack() as st2:
        old = fw.stack
        fw.stack = st2
        wq = fw.sbuf("wq", [128, 8, 2048], BF16)
        for k in range(8):
            fw.dma(wq[:, k, :], wq_d[k * 128:(k + 1) * 128, :], writes=[wq], q="pool")
        skT = fw.sbuf("skT", [128, 16, 128], F32)
        fw.dma(skT[:], skT_d.rearrange("hp q k -> q hp k"), writes=[skT])
        ht = [fw.sbuf("r_ht%d" % i, [128, D], F32) for i in range(2)]
        hTs = [fw.sbuf("r_hT%d" % i, [128, 8, 128], BF16) for i in range(2)]
        qTs = [fw.sbuf("r_qT%d" % i, [128, 16, 128], F32) for i in range(2)]
        ssbs = [fw.sbuf("r_ssb%d" % i, [128, 16, 128], F32) for i in range(2)]
        rt = [fw.sbuf("r_rt%d" % i, [128, RW], F32) for i in range(2)]
        tv = fw.sbuf("r_tv", [128, 16, 16], F32)
        tiu = fw.sbuf("r_tiu", [128, 16, 16], U32)
        tif = fw.sbuf("r_tif", [128, 16, 16], F32)
        scr = fw.sbuf("r_scr", [128, 16, 128], F32)
        cand = fw.sbuf("r_cand", [128, 8, 256], F32)
        scr2 = fw.sbuf("r_scr2", [128, 8, 256], F32)
        ct = fw.sbuf("r_ct", [128, 8, 16], F32)
        ciu = fw.sbuf("r_ciu", [128, 8, 16], U32)
        jku = fw.sbuf("r_jku", [128, 2, 128], U32)
        jkf = fw.sbuf("r_jkf", [128, 2, 128], F32)
        ohs = [fw.sbuf("r_oh%d" % i, [128, 8, 16, 16], F32) for i in range(2)]
        sm = fw.sbuf("r_sm", [128, 8, 16], F32)
        zz = fw.sbuf("r_zz", [128, 16], F32)

        tvs = [Buf(tv.t, "tv%d" % i) for i in range(16)]
        tius = [Buf(tiu.t, "tiu%d" % i) for i in range(16)]
        scrs = [Buf(scr.t, "scr%d" % i) for i in range(16)]
        cts = [Buf(ct.t, "ct%d" % i) for i in range(8)]
        cius = [Buf(ciu.t, "ciu%d" % i) for i in range(8)]
        scr2s = [Buf(scr2.t, "scr2%d" % i) for i in range(8)]
        fw.dma(ht[0][:], h_d[0:128, :], writes=[ht[0]])
        for ti in range(ntiles):
            htt = ht[ti % 2]
            rtt = rt[ti % 2]
            hT, qT, ssb = hTs[ti % 2], qTs[ti % 2], ssbs[ti % 2]
            if ti + 1 < ntiles:
                fw.dma(ht[(ti + 1) % 2][:], h_d[(ti + 1) * 128:(ti + 2) * 128, :], writes=[ht[(ti + 1) % 2]])
            transpose_to(fw, hT, lambda j, g: hT[:, j:j + g, :], htt, lambda j: htt[:, j * 128:(j + 1) * 128], 8, ident)
            for g4 in range(4):
                ps = fw.next_ps()
                for a in range(4):
                    hp = g4 * 4 + a
                    for k in range(8):
                        fw.op("pe", lambda e: e.matmul(ps[:, a * 128:(a + 1) * 128], lhsT=wq[:, k, hp * 128:(hp + 1) * 128],
                                                       rhs=hT[:, k, :], start=(k == 0), stop=(k == 7)),
                              reads=[wq, hT], writes=[ps])
                if g4 % 2 == 0:
                    fw.op("act", lambda e: e.copy(out=qT[:, g4 * 4:g4 * 4 + 4, :], in_=ps[:, :].rearrange("p (a n) -> p a n", a=4)),
                          reads=[ps], writes=[qT])
                else:
                    fw.op("dve", lambda e: e.tensor_copy(out=qT[:, g4 * 4:g4 * 4 + 4, :], in_=ps[:, :].rearrange("p (a n) -> p a n", a=4)),
                          reads=[ps], writes=[qT])
            for g4 in range(4):
                ps = fw.next_ps()
                for a in range(4):
                    hp = g4 * 4 + a
                    fw.op("pe", lambda e: e.matmul(ps[:, a * 128:(a + 1) * 128], lhsT=qT[:, hp, :], rhs=skT[:, hp, :],
                                                   start=True, stop=True), reads=[qT, skT], writes=[ps])
                fw.op("act", lambda e: e.copy(out=ssb[:, g4 * 4:g4 * 4 + 4, :], in_=ps[:, :].rearrange("p (a k) -> p a k", a=4)),
                      reads=[ps], writes=[ssb])
            for hp in range(16):
                fw.op("dve", lambda e: e.max(out=tv[:, hp, 0:8], in_=ssb[:, hp, :]), reads=[ssb], writes=[tvs[hp]])
            for hp in range(16):
                fw.op("dve", lambda e: e.max_index(out=tiu[:, hp, 0:8], in_max=tv[:, hp, 0:8], in_values=ssb[:, hp, :]),
                      reads=[ssb, tvs[hp]], writes=[tius[hp]])
            for hp in range(16):
                fw.op("dve", lambda e: e.match_replace(out=scr[:, hp, :], in_to_replace=tv[:, hp, 0:8], in_values=ssb[:, hp, :],
                                                       imm_value=-1e30), reads=[ssb, tvs[hp]], writes=[scrs[hp]])
            for hp in range(16):
                fw.op("dve", lambda e: e.max(out=tv[:, hp, 8:16], in_=scr[:, hp, :]), reads=[scrs[hp]], writes=[tvs[hp]])
            for hp in range(16):
                fw.op("dve", lambda e: e.max_index(out=tiu[:, hp, 8:16], in_max=tv[:, hp, 8:16], in_values=scr[:, hp, :]),
                      reads=[scrs[hp], tvs[hp]], writes=[tius[hp]])
            fw.op("pool", lambda e: e.tensor_copy(out=tif[:, :, :], in_=tiu[:, :, :]), reads=tius, writes=[tif])
            tv4 = tv[:, :, :].rearrange("p (h t) j -> p h t j", t=2)
            a_ = tv4[:, :, 0, :]
            b_ = tv4[:, :, 1, :]
            tif4 = tif[:, :, :].rearrange("p (h t) j -> p h t j", t=2)
            fw.op("dve", lambda e: e.tensor_tensor(out=cand[:, :, :].rearrange("p h (j k) -> p h j k", j=16),
                                                   in0=a_.unsqueeze(3).broadcast_to([128, 8, 16, 16]),
                                                   in1=b_.unsqueeze(2).broadcast_to([128, 8, 16, 16]), op=ALU.add),
                  reads=tvs, writes=[cand])
            for h in range(8):
                fw.op("dve", lambda e: e.max(out=ct[:, h, 0:8], in_=cand[:, h, :]), reads=[cand], writes=[cts[h]])
            for h in range(8):
                fw.op("dve", lambda e: e.max_index(out=ciu[:, h, 0:8], in_max=ct[:, h, 0:8], in_values=cand[:, h, :]),
                      reads=[cand, cts[h]], writes=[cius[h]])
            for h in range(8):
                fw.op("dve", lambda e: e.match_replace(out=scr2[:, h, :], in_to_replace=ct[:, h, 0:8], in_values=cand[:, h, :],
                                                       imm_value=-1e30), reads=[cand, cts[h]], writes=[scr2s[h]])
            for h in range(8):
                fw.op("dve", lambda e: e.max(out=ct[:, h, 8:16], in_=scr2[:, h, :]), reads=[scr2s[h]], writes=[cts[h]])
            for h in range(8):
                fw.op("dve", lambda e: e.max_index(out=ciu[:, h, 8:16], in_max=ct[:, h, 8:16], in_values=scr2[:, h, :]),
                      reads=[scr2s[h], cts[h]], writes=[cius[h]])
            ci2 = ciu[:, :, :].rearrange("p h r -> p (h r)")
            fw.op("dve", lambda e: e.tensor_single_scalar(out=jku[:, 0, :], in_=ci2, scalar=4, op=ALU.logical_shift_right),
                  reads=cius, writes=[jku])
            fw.op("dve", lambda e: e.tensor_single_scalar(out=jku[:, 1, :], in_=ci2, scalar=15, op=ALU.bitwise_and),
                  reads=cius, writes=[jku])
            fw.op("pool", lambda e: e.tensor_copy(out=jkf[:, :, :], in_=jku[:, :, :]), reads=[jku], writes=[jkf])
            B4 = [128, 8, 16, 16]
            for t in range(2):
                sel = jkf[:, t, :].rearrange("p (h r) -> p h r", h=8)
                oht = ohs[t]
                fw.op("dve", lambda e: e.tensor_tensor(out=oht[:, :, :, :], in0=sel.unsqueeze(3).broadcast_to(B4),
                                                       in1=iota16.unsqueeze(1).unsqueeze(1).broadcast_to(B4), op=ALU.is_equal),
                      reads=[jkf, cst], writes=[oht])
                fw.op("pool", lambda e: e.tensor_tensor(out=oht[:, :, :, :], in0=oht[:, :, :, :],
                                                        in1=tif4[:, :, t, :].unsqueeze(2).broadcast_to(B4), op=ALU.mult),
                      reads=[oht, tif], writes=[oht])
                fw.op("dve", lambda e: e.tensor_reduce(out=rtt[:, t * 128:(t + 1) * 128].rearrange("p (h r) -> p h r", h=8),
                                                       in_=oht[:, :, :, :], axis=AX.X, op=ALU.add),
                      reads=[oht], writes=[rtt])
            fw.op("dve", lambda e: e.tensor_tensor(out=sm[:, :, :], in0=ct[:, :, :], in1=ct[:, :, 0:1].broadcast_to([128, 8, 16]),
                                                   op=ALU.subtract), reads=cts, writes=[sm])
            fw.op("act", lambda e: e.activation(out=sm[:, :, :], in_=sm[:, :, :], func=AF.Exp), reads=[sm], writes=[sm])
            fw.op("dve", lambda e: e.tensor_reduce(out=zz[:, 0:8], in_=sm[:, :, :], axis=AX.X, op=ALU.add), reads=[sm], writes=[zz])
            fw.op("dve", lambda e: e.reciprocal(out=zz[:, 8:16], in_=zz[:, 0:8]), reads=[zz], writes=[zz])
            fw.op("dve", lambda e: e.tensor_tensor(out=rtt[:, 256:384].rearrange("p (h r) -> p h r", h=8), in0=sm[:, :, :],
                                                   in1=zz[:, 8:16].unsqueeze(2).broadcast_to([128, 8, 16]), op=ALU.mult),
                  reads=[sm, zz], writes=[rtt])
            fw.dma(rt_d[ti * 128:(ti + 1) * 128, :], rtt[:, :], reads=[rtt])
        fw.barrier()
        fw.stack = old


def precast_tables(fw, uh_d, vh_d, ub_d, vb_d):
    for c in range(128):
        fw.dma(ub_d[c], uh_d[c], q="pool")
        fw.dma(vb_d[c], vh_d[c], q="pool")


def peer_dense(fw, nc, cst, h_d, rt_d, ub_d, vb_d, lng_row, lnb_row, out_d, ntiles, NTB=2):
    ident = cst[:, 0:128]
    iota_i = cst[:, 640:768]
    TB = NTB * 128
    NS = 8
    with ExitStack() as st2:
        old = fw.stack
        fw.stack = st2
        gam = load_bcast(fw, "d_gam", lng_row, D)
        bet = load_bcast(fw, "d_bet", lnb_row, D)
        hprep = fw.sbuf("d_hprep", [128, D], F32)
        hT = [fw.sbuf("d_hT%d" % i, [128, 8, TB], BF16) for i in range(2)]
        rt = [fw.sbuf("d_rt%d" % i, [128, RW], F32) for i in range(2)]
        IT = [fw.sbuf("d_IT%d" % i, [128, 3, 128], F32) for i in range(2)]
        PTg = [fw.sbuf("d_PT%d" % i, [128, NS, 128], BF16) for i in range(2)]
        QTg = [fw.sbuf("d_QT%d" % i, [128, NS, 128], BF16) for i in range(2)]
        Cb = [fw.sbuf("d_Cb%d" % i, [128, TB * 128], BF16) for i in range(2)]
        NB = 6
        Ub = [fw.sbuf("d_U%d" % i, [128, 8, 128], BF16) for i in range(NB)]
        Vb = [fw.sbuf("d_V%d" % i, [128, D], BF16) for i in range(NB)]
        SK = 2
        NR = SK + 1
        G = [fw.sbuf("d_G%d" % i, [128, TB], BF16) for i in range(NR)]
        CA = [fw.sbuf("d_CA%d" % i, [128, TB], BF16) for i in range(NR)]
        ys = [fw.sbuf("d_y%d" % i, [128, D], F32) for i in range(2)]
        W = dict(bst=fw.sbuf("d_bst", [128, 2, 6], F32), mv=fw.sbuf("d_mv", [128, 4], F32))
        allps = fw.PS
        ACC = allps[0:4]
        fw.PS = allps[4:8]
        fw.ps_i = 0
        nblk = ntiles // NTB
        NG = 128 // NS
        NSTEP = NTB * NG

        def cbuild_steps(b):
            par = b % 2
            Cb3 = Cb[par][:, :].rearrange("p (n i) -> p n i", i=128)
            steps = []

            def dma_step(t):
                def f():
                    ti = b * NTB + t
                    fw.dma(hprep[:], h_d[ti * 128:(ti + 1) * 128, :], writes=[hprep])
                    fw.dma(rt[t % 2][:], rt_d[ti * 128:(ti + 1) * 128, :], writes=[rt[t % 2]])
                return f

            def tr_step(t):
                def f():
                    transpose_to(fw, hT[par], lambda j, g: hT[par][:, j:j + g, t * 128:(t + 1) * 128], hprep,
                                 lambda j: hprep[:, j * 128:(j + 1) * 128], 8, ident, evac=("act",))
                    transpose_to(fw, IT[t % 2], lambda j, g: IT[t % 2][:, j:j + g, :], rt[t % 2],
                                 lambda j: rt[t % 2][:, j * 128:(j + 1) * 128], 3, ident, evac=("act",))
                return f

            def build_p(q):
                def f():
                    t, g = q // NG, q % NG
                    it = IT[t % 2]
                    n0 = g * NS
                    Bs = [128, NS, 128]
                    fw.op("dve", lambda e: e.tensor_tensor(out=PTg[q % 2][:, :, :], in0=iota_i.unsqueeze(1).broadcast_to(Bs),
                                                           in1=it[:, 0, n0:n0 + NS].unsqueeze(2).broadcast_to(Bs), op=ALU.is_equal),
                          reads=[cst, it], writes=[PTg[q % 2]])
                    fw.op("pool", lambda e: e.tensor_tensor(out=PTg[q % 2][:, :, :], in0=PTg[q % 2][:, :, :],
                                                            in1=it[:, 2, n0:n0 + NS].unsqueeze(2).broadcast_to(Bs), op=ALU.mult),
                          reads=[PTg[q % 2], it], writes=[PTg[q % 2]])
                return f

            def build_q(q):
                def f():
                    t, g = q // NG, q % NG
                    it = IT[t % 2]
                    n0 = g * NS
                    Bs = [128, NS, 128]
                    fw.op("dve", lambda e: e.tensor_tensor(out=QTg[q % 2][:, :, :], in0=iota_i.unsqueeze(1).broadcast_to(Bs),
                                                           in1=it[:, 1, n0:n0 + NS].unsqueeze(2).broadcast_to(Bs), op=ALU.is_equal),
                          reads=[cst, it], writes=[QTg[q % 2]])
                return f

            def mm_step(q):
                def f():
                    t, g = q // NG, q % NG
                    for g4 in range(NS // 4):
                        ps = fw.next_ps()
                        for a in range(4):
                            n = g4 * 4 + a
                            fw.op("pe", lambda e: e.matmul(ps[:, a * 128:(a + 1) * 128], lhsT=PTg[q % 2][:, n, :], rhs=QTg[q % 2][:, n, :],
                                                           start=True, stop=True), reads=[PTg[q % 2], QTg[q % 2]], writes=[ps])
                        nb = t * 128 + g * NS + g4 * 4
                        fw.op("act", lambda e: e.copy(out=Cb3[:, nb:nb + 4, :], in_=ps[:, :].rearrange("p (a i) -> p a i", a=4)),
                              reads=[ps], writes=[Cb[par]])
                return f

            sched = {}
            pos = 0
            for t in range(NTB):
                sched.setdefault(pos, []).append(dma_step(t))
                sched.setdefault(pos + 1, []).append(tr_step(t))
                for g in range(NG):
                    q = t * NG + g
                    sched.setdefault(pos + 2 + 2 * g, []).append(build_p(q))
                    sched.setdefault(pos + 3 + 2 * g, []).append(build_q(q))
                    sched.setdefault(pos + 4 + 2 * g, []).append(mm_step(q))
                pos += 2 * NG
            nsteps = max(sched) + 1
            for p in range(nsteps):
                fl = sched.get(p, [])
                steps.append(lambda fl=fl: [f() for f in fl])
            return steps

        def load_chunk(c):
            fw.dma(Ub[c % NB][:], ub_d[c].rearrange("p (k i) -> p k i", k=8), writes=[Ub[c % NB]])
            fw.dma(Vb[c % NB][:], vb_d[c], writes=[Vb[c % NB]])

        for st_ in cbuild_steps(0):
            st_()
        for bi in range(nblk):
            par = bi % 2
            Cbc = Cb[par][:, :].rearrange("p (n i) -> p i n", i=128)
            nxt = cbuild_steps(bi + 1) if bi + 1 < nblk else []
            every = max(1, 120 // max(1, len(nxt))) if nxt else 0
            for c in range(NB):
                load_chunk(c)

            def front(c):
                U = Ub[c % NB]
                ps = fw.next_ps()
                for k in range(8):
                    fw.op("pe", lambda e: e.matmul(ps[:, 0:TB], lhsT=U[:, k, :], rhs=hT[par][:, k, :], start=(k == 0), stop=(k == 7)),
                          reads=[U, hT[par]], writes=[ps])
                fw.op("act", lambda e: e.activation(out=G[c % NR][:, :], in_=ps[:, 0:TB], func=AF.Gelu), reads=[ps], writes=[G[c % NR]])
                fw.op("dve", lambda e: e.tensor_tensor(out=CA[c % NR][:, :], in0=G[c % NR][:, :], in1=Cbc[:, c, :], op=ALU.mult),
                      reads=[G[c % NR], Cb[par]], writes=[CA[c % NR]])

            for c in range(SK):
                front(c)
            si = 0
            for c in range(128):
                if c + SK < 128:
                    front(c + SK)
                V = Vb[c % NB]
                ca = CA[c % NR]
                for t in range(NTB):
                    for hf in range(2):
                        acc = ACC[t * 2 + hf]
                        fw.op("pe", lambda e: e.matmul(acc[:, :], lhsT=ca[:, t * 128:(t + 1) * 128], rhs=V[:, hf * 512:(hf + 1) * 512],
                                                       start=(c == 0), stop=(c == 127)), reads=[ca, V], writes=[acc])
                if c + NB < 128:
                    load_chunk(c + NB)
                if nxt and c % every == every - 1 and si < len(nxt):
                    nxt[si]()
                    si += 1
            while si < len(nxt):
                nxt[si]()
                si += 1
            for t in range(NTB):
                ti = bi * NTB + t
                fw.dma(ys[t % 2][:], h_d[ti * 128:(ti + 1) * 128, :], writes=[ys[t % 2]])
            for t in range(NTB):
                ti = bi * NTB + t
                y = ys[t % 2]
                for hf in range(2):
                    acc = ACC[t * 2 + hf]
                    fw.op("dve", lambda e: e.scalar_tensor_tensor(out=y[:, hf * 512:(hf + 1) * 512], in0=y[:, hf * 512:(hf + 1) * 512],
                                                                  scalar=ALPHA, in1=acc[:, :], op0=ALU.mult, op1=ALU.add),
                          reads=[y, acc], writes=[y])
                layer_norm(fw, y, gam, bet, W)
                fw.dma(out_d[ti * 128:(ti + 1) * 128, :], y[:, :], reads=[y], q="pool")
        fw.barrier()
        fw.PS = allps
        fw.ps_i = 0
        fw.stack = old


_CACHE = {}
_DBG = {}


def kernel(**inputs):
    n = 8
    stop_after = int(inputs.pop("_stop_after", 4))
    start_at = int(inputs.pop("_start_at", 1))
    ntiles = int(inputs.pop("_ntiles", NT))
    ntb = int(inputs.pop("_ntb", 2))
    cores = inputs.pop("_cores", list(range(n)))
    key = (stop_after, ntiles, start_at, ntb)
    if key not in _CACHE:
        _CACHE[key] = build_program(stop_after, ntiles, start_at, ntb)
    nc = _CACHE[key]
    f = lambda a: np.ascontiguousarray(np.asarray(a, dtype=np.float32))
    wg = np.zeros((32, 384), np.float32)
    wg[0:16] = inputs["a_w_gate2"][0]
    wg[16] = inputs["a_b_gate"][0]
    shared = {
        "a_w_in": f(inputs["a_w_in"][0]),
        "a_wg": wg,
        "a_hg": f(inputs["a_head_g"][0]).reshape(1, 768),
        "a_wkv": f(inputs["a_w_mem_kv"][0]),
        "a_wo": f(inputs["a_w_out"][0]),
        "ln_g": f(inputs["ln_g"]).reshape(4, D),
        "ln_b": f(inputs["ln_b"]).reshape(4, D),
        "consts": make_consts(),
        "sbkv": f(inputs["sb_w_kv"]),
        "b_w_in": f(inputs["b_w_in"][0]),
        "b_wkv": f(inputs["b_w_mem_kv"][0]),
        "b_wo": f(inputs["b_w_out"][0]),
    }
    for l in range(2):
        shared["wq%d" % l] = f(inputs["peer_w_q"][l])
        shared["skT%d" % l] = f(np.asarray(inputs["peer_subkeys"][l]).reshape(16, 128, 128).transpose(0, 2, 1))
        u = np.asarray(inputs["peer_u"][l], dtype=np.float32).reshape(128, 128, 8, 128)
        shared["uh%d" % l] = np.ascontiguousarray(u.transpose(1, 3, 2, 0)).reshape(128, 128, D)
        v = np.asarray(inputs["peer_v"][l], dtype=np.float32).reshape(128, 128, D)
        shared["vh%d" % l] = np.ascontiguousarray(v.transpose(1, 0, 2))
    in_maps = []
    for b in cores:
        m = dict(shared)
        m["x"] = f(inputs["x"][b])
        m["mem"] = f(inputs["mem"][b])
        in_maps.append(m)
    res = run_bass_kernel_spmd(nc, in_maps, core_ids=list(range(len(cores))))
    return np.stack([r["out"] for r in res.results], axis=0)
```
